# Optimizing a Trainium2 kernel written in Bass

```python
import jax, jax.numpy as jnp
from jax import lax
import numpy as np

D_MODEL = 4096
BATCH = 8
SEQ = 2048
DEPTH = 2

HEAD_DIM = 128
D_LRU = D_MODEL // 4
LRU_BLOCKS = D_LRU // HEAD_DIM
LRU_BLOCK = D_LRU // LRU_BLOCKS
CONV_WIDTH = 4
LRU_C = 8.0
N_FOX = D_MODEL // 4 // HEAD_DIM
D_FOX = N_FOX * HEAD_DIM
D_DIL = D_MODEL - D_LRU - D_FOX
N_DIL = D_DIL // HEAD_DIM
D_MIX = D_LRU + D_FOX + D_DIL
D_IN = 2 * D_LRU + 3 * D_FOX + N_FOX + 3 * D_DIL
D_FF = 11008
ROPE_THETA = 500000.0
ROPE_DIM = HEAD_DIM // 4
DILATED_PATTERNS = ((128, 1), (512, 4), (2048, 16))
Q_BLOCK = 128
EPS = 1e-6

kernel_name = "hymba_style_lru_fox_dilated_macaron"


def rmsnorm(x, g):
    x32 = x.astype(jnp.float32)
    y = x32 * lax.rsqrt(jnp.mean(x32 * x32, axis=-1, keepdims=True) + EPS)
    return (y * g.astype(jnp.float32)).astype(x.dtype)


def swiglu(x, w_in, w_out):
    gu = x @ w_in
    gate, up = gu[..., :D_FF], gu[..., D_FF:]
    return (jax.nn.silu(gate) * up) @ w_out


def rope_tables(seq):
    inv = 1.0 / (ROPE_THETA ** (jnp.arange(0, ROPE_DIM, 2, dtype=jnp.float32) / ROPE_DIM))
    ang = jnp.arange(seq, dtype=jnp.float32)[:, None] * inv[None, :]
    return jnp.cos(ang), jnp.sin(ang)


def apply_partial_rope(x, cos, sin):
    half = ROPE_DIM // 2
    x1 = x[..., :half].astype(jnp.float32)
    x2 = x[..., half:ROPE_DIM].astype(jnp.float32)
    c = cos[None, :, None, :]
    s = sin[None, :, None, :]
    rot = jnp.concatenate([x1 * c - x2 * s, x2 * c + x1 * s], axis=-1).astype(x.dtype)
    return jnp.concatenate([rot, x[..., ROPE_DIM:]], axis=-1)


def causal_depthwise_conv(x, w, b):
    S = x.shape[1]
    xp = jnp.pad(x, ((0, 0), (CONV_WIDTH - 1, 0), (0, 0)))
    y = b
    for j in range(CONV_WIDTH):
        y = y + xp[:, j:j + S] * w[j]
    return y


def rg_lru(x, w_a, b_a, w_x, b_x, lam):
    B, S, C = x.shape
    xb = x.reshape(B, S, LRU_BLOCKS, LRU_BLOCK)
    r = jax.nn.sigmoid(jnp.einsum('bsgi,gij->bsgj', xb, w_a).reshape(B, S, C) + b_a)
    i = jax.nn.sigmoid(jnp.einsum('bsgi,gij->bsgj', xb, w_x).reshape(B, S, C) + b_x)
    log_a = -LRU_C * r.astype(jnp.float32) * jax.nn.softplus(-lam.astype(jnp.float32))
    a = jnp.exp(log_a)
    u = jnp.sqrt(-jnp.expm1(2.0 * log_a)) * (i * x).astype(jnp.float32)

    def combine(c1, c2):
        a1, b1 = c1
        a2, b2 = c2
        return a1 * a2, a2 * b1 + b2

    _, h = lax.associative_scan(combine, (a, u), axis=1)
    return h.astype(x.dtype)


def forgetting_attention(q, k, v, log_f):
    B, S, H, hd = q.shape
    nb = S // Q_BLOCK
    cum = jnp.cumsum(log_f, axis=1).transpose(0, 2, 1)
    qb = q.reshape(B, nb, Q_BLOCK, H, hd).transpose(1, 0, 2, 3, 4)
    cb = cum.reshape(B, H, nb, Q_BLOCK).transpose(2, 0, 1, 3)
    k_pos = jnp.arange(S)
    scale = hd ** -0.5

    def one_block(args):
        blk, q_i, c_i = args
        s = jnp.einsum('bqhe,bkhe->bhqk', q_i, k).astype(jnp.float32) * scale
        s = s + c_i[..., None] - cum[:, :, None, :]
        q_pos = blk * Q_BLOCK + jnp.arange(Q_BLOCK)
        s = jnp.where((k_pos[None, :] <= q_pos[:, None])[None, None], s, -jnp.inf)
        p = jax.nn.softmax(s, axis=-1)
        return jnp.einsum('bhqk,bkhe->bqhe', p.astype(v.dtype), v)

    out = lax.map(one_block, (jnp.arange(nb), qb, cb))
    return out.transpose(1, 0, 2, 3, 4).reshape(B, S, H, hd)


def dilated_window_attention(q, k, v, window, dilation):
    B, S, H, hd = q.shape
    steps = window // dilation
    L = S // dilation
    nb = -(-L // Q_BLOCK)
    Lp = nb * Q_BLOCK

    def to_strided(t):
        t = t.reshape(B, L, dilation, H, hd).transpose(0, 2, 1, 3, 4)
        return jnp.pad(t, ((0, 0), (0, 0), (0, Lp - L), (0, 0), (0, 0)))

    def key_band(t):
        t = jnp.pad(to_strided(t), ((0, 0), (0, 0), (Q_BLOCK, 0), (0, 0), (0, 0)))
        t = t.reshape(B, dilation, nb + 1, Q_BLOCK, H, hd)
        return jnp.concatenate([t[:, :, :-1], t[:, :, 1:]], axis=3)

    qs = to_strided(q).reshape(B, dilation, nb, Q_BLOCK, H, hd)
    kw = key_band(k)
    vw = key_band(v)
    s = jnp.einsum('brnqhe,brnkhe->brnhqk', qs, kw).astype(jnp.float32) * (hd ** -0.5)
    blk = jnp.arange(nb)[:, None, None]
    qpos = jnp.arange(Q_BLOCK)[None, :, None]
    kpos = jnp.arange(2 * Q_BLOCK)[None, None, :] - Q_BLOCK
    rel = qpos - kpos
    valid = (rel >= 0) & (rel <= steps) & (blk * Q_BLOCK + kpos >= 0)
    s = jnp.where(valid[None, None, :, None], s, -jnp.inf)
    m = jnp.max(s, axis=-1, keepdims=True)
    e = jnp.exp(s - m)
    den = jnp.sum(e, axis=-1, keepdims=True)
    p = e / den
    lse = (m + jnp.log(den))[..., 0]
    o = jnp.einsum('brnhqk,brnkhe->brnqhe', p.astype(v.dtype), vw)
    o = o.reshape(B, dilation, Lp, H, hd)[:, :, :L].transpose(0, 2, 1, 3, 4).reshape(B, S, H, hd)
    lse = lse.transpose(0, 1, 2, 4, 3).reshape(B, dilation, Lp, H)[:, :, :L]
    lse = lse.transpose(0, 2, 1, 3).reshape(B, S, H)
    return o, lse


def dilated_mixture(q, k, v):
    outs, lses = [], []
    for window, dilation in DILATED_PATTERNS:
        o, l = dilated_window_attention(q, k, v, window, dilation)
        outs.append(o)
        lses.append(l)
    w = jax.nn.softmax(jnp.stack(lses, axis=0), axis=0)
    out = sum(w[i][..., None] * outs[i].astype(jnp.float32) for i in range(len(outs)))
    return out.astype(q.dtype)


def hybrid_mixer(h, w_in, conv_w, conv_b, lru_w_a, lru_b_a, lru_w_x, lru_b_x, lru_lam,
                 fox_b_f, out_norm_lru, out_norm_fox, out_norm_dil, w_out, cos, sin):
    B, S, _ = h.shape
    proj = h @ w_in
    sizes = (D_LRU, D_LRU, D_FOX, D_FOX, D_FOX, N_FOX, D_DIL, D_DIL, D_DIL)
    points = np.cumsum(sizes)[:-1].tolist()
    xa, ga, qf, kf, vf, ff, qd, kd, vd = jnp.split(proj, points, axis=-1)

    xa = causal_depthwise_conv(xa, conv_w, conv_b)
    y_a = rg_lru(xa, lru_w_a, lru_b_a, lru_w_x, lru_b_x, lru_lam) * jax.nn.gelu(ga)

    log_f = jax.nn.log_sigmoid(ff.astype(jnp.float32) + fox_b_f.astype(jnp.float32))
    hd4 = lambda t, n: t.reshape(B, S, n, HEAD_DIM)
    y_b = forgetting_attention(hd4(qf, N_FOX), hd4(kf, N_FOX), hd4(vf, N_FOX), log_f).reshape(B, S, D_FOX)

    qd = apply_partial_rope(hd4(qd, N_DIL), cos, sin)
    kd = apply_partial_rope(hd4(kd, N_DIL), cos, sin)
    y_c = dilated_mixture(qd, kd, hd4(vd, N_DIL)).reshape(B, S, D_DIL)

    y = jnp.concatenate([rmsnorm(y_a, out_norm_lru), rmsnorm(y_b, out_norm_fox), rmsnorm(y_c, out_norm_dil)], axis=-1)
    return y @ w_out


def setup_inputs(seed: int = 0) -> dict:
    key = jax.random.key(seed)
    ks = jax.random.split(key, 24)
    L = DEPTH

    def nrm(k, shape, fan_in):
        return jax.random.normal(k, shape, jnp.float32) * (fan_in ** -0.5)

    def gain(k, shape):
        return 1.0 + 0.05 * jax.random.normal(k, shape, jnp.float32)

    def bias(k, shape):
        return 0.01 * jax.random.normal(k, shape, jnp.float32)

    u = jax.random.uniform(ks[12], (L, D_LRU), jnp.float32, minval=0.9, maxval=0.999)
    s = u ** (1.0 / LRU_C)
    lru_lam = jnp.log(s) - jnp.log1p(-s)
    return {
        "x": jax.random.normal(ks[0], (BATCH, SEQ, D_MODEL), jnp.float32),
        "ffn1_norm": gain(ks[1], (L, D_MODEL)),
        "ffn1_w_in": nrm(ks[2], (L, D_MODEL, 2 * D_FF), D_MODEL),
        "ffn1_w_out": nrm(ks[3], (L, D_FF, D_MODEL), D_FF),
        "mix_norm": gain(ks[4], (L, D_MODEL)),
        "mix_w_in": nrm(ks[5], (L, D_MODEL, D_IN), D_MODEL),
        "conv_w": nrm(ks[6], (L, CONV_WIDTH, D_LRU), CONV_WIDTH),
        "conv_b": bias(ks[7], (L, D_LRU)),
        "lru_w_a": nrm(ks[8], (L, LRU_BLOCKS, LRU_BLOCK, LRU_BLOCK), LRU_BLOCK),
        "lru_b_a": bias(ks[9], (L, D_LRU)),
        "lru_w_x": nrm(ks[10], (L, LRU_BLOCKS, LRU_BLOCK, LRU_BLOCK), LRU_BLOCK),
        "lru_b_x": bias(ks[11], (L, D_LRU)),
        "lru_lam": lru_lam,
        "fox_b_f": 3.0 + 0.1 * jax.random.normal(ks[13], (L, N_FOX), jnp.float32),
        "out_norm_lru": gain(ks[14], (L, D_LRU)),
        "out_norm_fox": gain(ks[15], (L, D_FOX)),
        "out_norm_dil": gain(ks[16], (L, D_DIL)),
        "mix_w_out": nrm(ks[17], (L, D_MIX, D_MODEL), D_MIX),
        "ffn2_norm": gain(ks[18], (L, D_MODEL)),
        "ffn2_w_in": nrm(ks[19], (L, D_MODEL, 2 * D_FF), D_MODEL),
        "ffn2_w_out": nrm(ks[20], (L, D_FF, D_MODEL), D_FF),
        "final_norm": gain(ks[21], (D_MODEL,)),
    }


def reference(x, ffn1_norm, ffn1_w_in, ffn1_w_out, mix_norm, mix_w_in, conv_w, conv_b,
              lru_w_a, lru_b_a, lru_w_x, lru_b_x, lru_lam, fox_b_f,
              out_norm_lru, out_norm_fox, out_norm_dil, mix_w_out,
              ffn2_norm, ffn2_w_in, ffn2_w_out, final_norm):
    cos, sin = rope_tables(x.shape[1])
    h = x
    for l in range(DEPTH):
        h = h + 0.5 * swiglu(rmsnorm(h, ffn1_norm[l]), ffn1_w_in[l], ffn1_w_out[l])
        h = h + hybrid_mixer(rmsnorm(h, mix_norm[l]), mix_w_in[l], conv_w[l], conv_b[l],
                             lru_w_a[l], lru_b_a[l], lru_w_x[l], lru_b_x[l], lru_lam[l], fox_b_f[l],
                             out_norm_lru[l], out_norm_fox[l], out_norm_dil[l], mix_w_out[l], cos, sin)
        h = h + 0.5 * swiglu(rmsnorm(h, ffn2_norm[l]), ffn2_w_in[l], ffn2_w_out[l])
    return rmsnorm(h, final_norm)
```

```python
import contextlib
import os
_DBG = os.environ.get("MK_DBG", "")
import numpy as np
import concourse.bass as bass
import concourse.mybir as mybir
from concourse.bass_utils import run_bass_kernel_spmd

F32 = mybir.dt.float32
BF16 = mybir.dt.bfloat16
AF = mybir.ActivationFunctionType
ALU = mybir.AluOpType
AX = mybir.AxisListType

S = 2048
D = 4096
DFF = 11008
DEPTH = 2
HD = 128
D_LRU = 1024
D_FOX = 1024
N_FOX = 8
D_DIL = 2048
N_DIL = 16
D_IN = 11272
EPS = 1e-6
NCH = D // 128
FCH = DFF // 128
TB = 1024
NTB = S // TB
TT = TB // 128
NEG = -30000.0


_UID = [0]


def U(name):
    _UID[0] += 1
    return f"{name}_{_UID[0]}"


class Buf:
    __slots__ = ("name", "w", "r")

    def __init__(self, name=""):
        self.name = name
        self.w = None
        self.r = {}


class TK:
    def __init__(self, nc, es, n_sp=24, n_pool=12):
        self.nc = nc
        self.eng = {"pe": nc.tensor, "act": nc.scalar, "dve": nc.vector, "pool": nc.gpsimd, "sp": nc.sync}
        self.prog = {}
        for e in ("pe", "act", "dve", "pool"):
            self.prog[e] = [es.enter_context(nc.semaphore("pg_" + e)), 0]
        self.waited = {e: {} for e in self.eng}
        self.dpool = {
            "sp": [[es.enter_context(nc.semaphore(f"dsp{i}")), 0] for i in range(n_sp)],
            "pool": [[es.enter_context(nc.semaphore(f"dpl{i}")), 0] for i in range(n_pool)],
        }
        self.drr = {"sp": 0, "pool": 0}
        self.nwaits = 0
        self.ninst = 0

    def _wait(self, eng, sem, val):
        key = id(sem)
        w = self.waited[eng]
        if w.get(key, 0) >= val:
            return
        w[key] = val
        self.eng[eng].wait_ge(sem[0], val)
        self.nwaits += 1

    def _deps(self, eng, reads, writes):
        for b in reads:
            if b.w is not None:
                self._dep1(eng, b.w)
        for b in writes:
            if b.w is not None:
                self._dep1(eng, b.w)
            for t in b.r.values():
                self._dep1(eng, t)

    def _dep1(self, eng, tok):
        sem, val, peng = tok
        if peng == "pe" and eng == "pe":
            return
        self._wait(eng, sem, val)

    def _reg(self, tok, reads, writes):
        key = tok[2] if tok[2] != "dma" else id(tok[0])
        for b in writes:
            b.w = tok
            b.r = {}
        for b in reads:
            b.r[key] = tok

    def op(self, eng, inst_fn, reads=(), writes=()):
        self._deps(eng, reads, writes)
        inst = inst_fn(self.eng[eng])
        p = self.prog[eng]
        p[1] += 1
        inst.then_inc(p[0], 1)
        self.ninst += 1
        self._reg((p, p[1], eng), reads, writes)
        return inst

    def dma(self, q, out, in_, reads=(), writes=(), **kw):
        self._deps(q, reads, writes)
        pool = self.dpool[q]
        i = self.drr[q]
        self.drr[q] = (i + 1) % len(pool)
        s = pool[i]
        if s[1] > 0:
            self._wait(q, s, s[1])
        inst = self.eng[q].dma_start(out=out, in_=in_, **kw)
        s[1] += 16
        inst.then_inc(s[0], 16)
        self.ninst += 1
        self._reg((s, s[1], "dma"), reads, writes)
        return inst

    def barrier(self, engs=("pe", "act", "dve", "pool", "sp")):
        for e in engs:
            for e2, p in self.prog.items():
                if e2 != e and p[1] > 0:
                    self._wait(e, p, p[1])
            for q in self.dpool.values():
                for s in q:
                    if s[1] > 0:
                        self._wait(e, s, s[1])


class Ring:
    def __init__(self, nc, es, name, shape, dtype, n):
        self.t = [es.enter_context(nc.sbuf_tensor(U(f"{name}{i}"), shape, dtype)) for i in range(n)]
        self.b = [Buf(f"{name}{i}") for i in range(n)]
        self.i = 0
        self.n = n

    def next(self):
        i = self.i
        self.i = (i + 1) % self.n
        return self.t[i], self.b[i]


class PS:
    def __init__(self, nc, es):
        self.t = [es.enter_context(nc.psum_tensor(f"psb{i}", [128, 512], F32)) for i in range(8)]
        self.b = [Buf(f"psb{i}") for i in range(8)]

    def ring(self, idx):
        return PRing(self, idx)


class PRing:
    def __init__(self, ps, idx):
        self.ps = ps
        self.idx = list(idx)
        self.i = 0

    def next(self):
        j = self.idx[self.i]
        self.i = (self.i + 1) % len(self.idx)
        return self.ps.t[j], self.ps.b[j]


def norm_transpose_block(nc, T, es, P, hsrc, hbufs_blk, blk, g_sb, xnT, xnT_b, ident_bf, cst, eps_sb, CB):
    hs_r = Ring(nc, es, "nt_hs", [128, D], F32, 2)
    xn_r = Ring(nc, es, "nt_xn", [128, D], BF16, 2)
    st_r = Ring(nc, es, "nt_st", [128, 4], F32, 2)
    tp_r = P.ring([0, 1, 2, 3])
    for tt in range(TT):
        r0 = blk * TB + tt * 128
        hs, hs_b = hs_r.next()
        xn, xn_b = xn_r.next()
        st, st_b = st_r.next()
        T.dma("sp", hs[:], hsrc[r0:r0 + 128, :], reads=hbufs_blk, writes=[hs_b])
        T.op("act", lambda e: e.activation(out=xn[:], in_=hs[:], func=AF.Square, accum_out=st[:, 0:1]),
             reads=[hs_b], writes=[xn_b, st_b])
        T.op("act", lambda e: e.activation(out=st[:, 1:2], in_=st[:, 0:1], func=AF.Sqrt, scale=1.0 / D, bias=eps_sb[:, 0:1]),
             reads=[st_b, CB], writes=[st_b])
        T.op("dve", lambda e: e.reciprocal(out=st[:, 2:3], in_=st[:, 1:2]), reads=[st_b], writes=[st_b])
        T.op("dve", lambda e: e.tensor_scalar(out=xn[:], in0=hs[:], scalar1=st[:, 2:3], scalar2=None,
                                              op0=ALU.mult), reads=[hs_b, st_b], writes=[xn_b])
        for cg in range(NCH // 8):
            tpf, tp_b = tp_r.next()
            tp = tpf[:].bitcast(BF16).rearrange("p (j c) -> p j c", c=128)
            for j in range(8):
                c = cg * 8 + j
                T.op("pe", lambda e: e.transpose(out=tp[:, j, :], in_=xn[:, c * 128:(c + 1) * 128],
                                                 identity=ident_bf[:]), reads=[xn_b, CB], writes=[tp_b])
            for j in range(8):
                c = cg * 8 + j
                eng = "act" if j % 2 == 0 else "dve"
                if eng == "act":
                    T.op("act", lambda e: e.activation(out=xnT[:, c, tt * 128:(tt + 1) * 128], in_=tp[:, j, :],
                                                       func=AF.Copy, scale=g_sb[:, c:c + 1]),
                         reads=[tp_b, cst], writes=[xnT_b[tt]])
                else:
                    T.op("dve", lambda e: e.tensor_scalar(out=xnT[:, c, tt * 128:(tt + 1) * 128], in0=tp[:, j, :],
                                                          scalar1=g_sb[:, c:c + 1], scalar2=None, op0=ALU.mult),
                         reads=[tp_b, cst], writes=[xnT_b[tt]])


def tm_proj_residual(nc, T, lhsT, rb, nk, w_rows, hsrc, hdst, hb_blk, blk, alpha, wr, ps_r, hp_r):
    r0 = blk * TB

    def epi(ct, get):
        c0 = ct * 256
        hp, hp_b = hp_r.next()
        hv = hsrc[r0:r0 + TB, c0:c0 + 256].rearrange("(t p) c -> p t c", p=128)
        ov = hdst[r0:r0 + TB, c0:c0 + 256].rearrange("(t p) c -> p t c", p=128)
        cb = hb_blk[ct]
        T.dma("sp", hp[:], hv, reads=[cb], writes=[hp_b])
        for tt in range(TT):
            ps, ps_b = get(tt)
            T.op("dve", lambda e: e.scalar_tensor_tensor(out=hp[:, tt, :], in0=ps, scalar=alpha, in1=hp[:, tt, :],
                                                         op0=ALU.mult, op1=ALU.add), reads=[ps_b, hp_b], writes=[hp_b])
        T.dma("sp", ov, hp[:], reads=[hp_b], writes=[cb])
    tm_proj(nc, T, lhsT, rb, nk, w_rows, D, wr, ps_r, epi)


def ffn_phase(nc, T, P, C, hsrc, hdst, hb, g_pc, w_in, w_out, nblk=NTB, parts=None, dbg=None):
    ident_bf, cst, eps_sb = C["ident_bf"], C["cb"], C["eps"]
    if parts is None:
        parts = [(0, 22), (22, 44), (44, 65), (65, 86)]
    with contextlib.ExitStack() as es:
        xnT = es.enter_context(nc.sbuf_tensor(U("f_xnT"), [128, NCH, TB], BF16))
        xnT_b = [Buf(f"xnT{t}") for t in range(TT)]
        maxp = max(b - a for a, b in parts)
        actT = es.enter_context(nc.sbuf_tensor(U("f_actT"), [128, maxp, TB], BF16))
        actT_b = [Buf(f"actT{i}") for i in range(maxp)]
        g_sb = es.enter_context(nc.sbuf_tensor(U("f_g"), [128, NCH], F32))
        g_b = Buf("g")
        T.dma("sp", g_sb[:], g_pc, writes=[g_b])
        wr = Ring(nc, es, "f_w", [128, 8, 256], BF16, 4)
        hp_r = Ring(nc, es, "f_hp", [128, TT, 256], F32, 2)
        sg_r = Ring(nc, es, "f_sg", [128, 512], F32, 3)
        for blk in range(nblk):
            with contextlib.ExitStack() as es2:
                norm_transpose_block(nc, T, es2, P, hsrc, hb[blk], blk, g_sb, xnT, xnT_b, ident_bf, g_b, eps_sb, cst)
                T.barrier()
            if dbg is not None:
                T.dma("sp", dbg, xnT[:], reads=xnT_b)
                return
            src = hsrc
            for (fa, fb) in parts:
                with contextlib.ExitStack() as es2:
                    ps_g = P.ring([0, 1, 2, 3])
                    ps_u = P.ring([4, 5, 6, 7])
                    f = fa
                    while f < fb:
                        nf = min(2, fb - f)
                        sets = {}
                        for gu, pr in ((0, ps_g), (1, ps_u)):
                            cbase = gu * DFF + f * 128
                            banks = [[pr.next() for th in range(2)] for fc in range(nf)]
                            sets[gu] = banks
                            for kg in range(NCH // 8):
                                wt, wt_b = wr.next()
                                T.dma("pool", wt[:, :, 0:nf * 128],
                                      w_in[kg * 1024:(kg + 1) * 1024, cbase:cbase + nf * 128].rearrange("(k p) c -> p k c", p=128),
                                      writes=[wt_b])
                                for k8 in range(8):
                                    k = kg * 8 + k8
                                    for fc in range(nf):
                                        for th in range(2):
                                            ps, ps_b = banks[fc][th]
                                            T.op("pe", lambda e: e.matmul(out=ps[:], lhsT=wt[:, k8, fc * 128:(fc + 1) * 128],
                                                                          rhs=xnT[:, k, th * 512:(th + 1) * 512],
                                                                          start=(k == 0), stop=(k == NCH - 1)),
                                                 reads=[wt_b] + xnT_b[th * 4:(th + 1) * 4], writes=[ps_b])
                        for fc in range(nf):
                            for th in range(2):
                                sg, sg_b = sg_r.next()
                                pg, pg_b = sets[0][fc][th]
                                pu, pu_b = sets[1][fc][th]
                                T.op("act", lambda e: e.activation(out=sg[:], in_=pg[:], func=AF.Silu),
                                     reads=[pg_b], writes=[sg_b])
                                T.op("dve", lambda e: e.tensor_tensor(out=actT[:, f - fa + fc, th * 512:(th + 1) * 512],
                                                                      in0=sg[:], in1=pu[:], op=ALU.mult),
                                     reads=[sg_b, pu_b], writes=[actT_b[f - fa + fc]])
                        f += nf
                with contextlib.ExitStack() as es2:
                    ps_r = P.ring(range(8))
                    tm_proj_residual(
                        nc, T, actT, lambda k, tt: actT_b[k], fb - fa,
                        lambda k0, k1, c0, c1: w_out[(fa + k0) * 128:(fa + k1) * 128, c0:c1].rearrange("(k p) c -> p k c", p=128),
                        src, hdst, hb[blk], blk, 0.5, wr, ps_r, hp_r)
                src = hdst
            T.barrier()


def tm_proj(nc, T, lhsT, rb, nk, w_rows, ncols, wr, ps_r, epi):
    KC = 8
    for ct in range(ncols // 256):
        pss = [ps_r.next() for _ in range(TT // 2)]
        ngrp = (nk + KC - 1) // KC
        for kg in range(ngrp):
            k0 = kg * KC
            k1 = min(nk, k0 + KC)
            wt, wt_b = wr.next()
            T.dma("pool", wt[:, 0:k1 - k0, :], w_rows(k0, k1, ct * 256, ct * 256 + 256), writes=[wt_b])
            for k in range(k0, k1):
                for tt in range(TT):
                    ps, ps_b = pss[tt // 2]
                    T.op("pe", lambda e: e.matmul(out=ps[:, (tt % 2) * 256:(tt % 2) * 256 + 256],
                                                  lhsT=lhsT[:, k, tt * 128:(tt + 1) * 128], rhs=wt[:, k - k0, :],
                                                  start=(k == 0 and tt % 2 == 0), stop=(k == nk - 1), skip_group_check=True),
                         reads=[wt_b, rb(k, tt)], writes=[ps_b])

        def get(tt):
            ps, ps_b = pss[tt // 2]
            return ps[:, (tt % 2) * 256:(tt % 2) * 256 + 256], ps_b
        epi(ct, get)


def mixer_proj_phase(nc, T, P, C, l, h, hb, prm, scr):
    w_in = prm["mix_w_in"][l]
    with contextlib.ExitStack() as es:
        xnT = es.enter_context(nc.sbuf_tensor(U("m_xnT"), [128, NCH, TB], BF16))
        xnT_b = [Buf(f"mxnT{t}") for t in range(TT)]
        g_sb = es.enter_context(nc.sbuf_tensor(U("m_g"), [128, NCH], F32))
        g_b = Buf("mg")
        T.dma("sp", g_sb[:], prm["mix_norm"][l], writes=[g_b])
        cs_sb = es.enter_context(nc.sbuf_tensor(U("m_cs"), [128, S // 128, 2, 2, 16], F32))
        cs_b = Buf("cs")
        T.dma("sp", cs_sb[:], C["rope"], writes=[cs_b])
        wr = Ring(nc, es, "m_w", [128, 8, 256], BF16, 4)
        st32 = Ring(nc, es, "m_st32", [128, TB], F32, 3)
        stq = Ring(nc, es, "m_stq", [128, TT, 256], BF16, 2)
        tmp = Ring(nc, es, "m_tmp", [128, 4, 2, 16], F32, 2)
        xs_r = Ring(nc, es, "m_xs", [128, 256], F32, 3)
        for blk in range(NTB):
            t0 = blk * TB
            with contextlib.ExitStack() as es2:
                norm_transpose_block(nc, T, es2, P, h, hb[blk], blk, g_sb, xnT, xnT_b, C["ident_bf"], g_b, C["eps"], C["cb"])
                T.barrier()
            ps_sets = [P.ring([0, 1, 2, 3]), P.ring([4, 5, 6, 7])]
            units = [(f * 128, 2, False) for f in range(0, 16, 2)] + [(5120, 1, True)]
            if "nofm" in _DBG:
                units = []
            if "noff" in _DBG:
                units = units[:-1]
            for ui, (cbase, nf, is_ff) in enumerate(units):
                pr = ps_sets[ui % 2]
                ncol = nf * 128
                nfc = nf
                banks = [[pr.next() for th in range(2)] for fc in range(nfc)]
                for kg in range(NCH // 8):
                    wt, wt_b = wr.next()
                    T.dma("pool", wt[:, :, 0:ncol],
                          w_in[kg * 1024:(kg + 1) * 1024, cbase:cbase + ncol].rearrange("(k p) c -> p k c", p=128), writes=[wt_b])
                    for k8 in range(8):
                        k = kg * 8 + k8
                        for fc in range(nfc):
                            m = 128
                            for th in range(2):
                                ps, ps_b = banks[fc][th]
                                T.op("pe", lambda e: e.matmul(out=ps[0:m, :], lhsT=wt[:, k8, fc * 128:fc * 128 + m],
                                                              rhs=xnT[:, k, th * 512:(th + 1) * 512],
                                                              start=(k == 0), stop=(k == NCH - 1)),
                                     reads=[wt_b] + xnT_b[th * 4:(th + 1) * 4], writes=[ps_b])
                for fc in range(nfc):
                    m = 8 if is_ff else 128
                    sg, sg_b = st32.next()
                    for th in range(2):
                        ps, ps_b = banks[fc][th]
                        eng = "act" if th == 0 else "dve"
                        if eng == "act":
                            T.op("act", lambda e: e.activation(out=sg[0:m, th * 512:(th + 1) * 512], in_=ps[0:m, :], func=AF.Copy),
                                 reads=[ps_b], writes=[sg_b])
                        else:
                            T.op("dve", lambda e: e.tensor_copy(out=sg[0:m, th * 512:(th + 1) * 512], in_=ps[0:m, :]),
                                 reads=[ps_b], writes=[sg_b])
                    if not is_ff:
                        r0 = cbase + fc * 128
                        T.dma("sp", scr["xgT"][r0:r0 + 128, t0:t0 + TB], sg[:], reads=[sg_b], writes=[scr["xgT_b"][r0 // 128]])
                    else:
                        T.dma("sp", scr["ffT"][:, t0:t0 + TB], sg[0:8, :], reads=[sg_b], writes=[scr["ffT_b"]])
            ps_r = P.ring(range(8))
            for (wc0, ncols, oc0, rope_cols) in (((2048, 3072, 0, 0), (5128, 6144, 3072, 4096)) if "notm" not in _DBG else ()):
                def epi(ct, get, wc0=wc0, oc0=oc0, rope_cols=rope_cols):
                    sq, sq_b = stq.next()
                    do_rope = ct * 256 < rope_cols and "norope" not in _DBG
                    for tt in range(TT):
                        ps, ps_b = get(tt)
                        if not do_rope:
                            if tt % 2 == 0:
                                T.op("act", lambda e: e.activation(out=sq[:, tt, :], in_=ps, func=AF.Copy), reads=[ps_b], writes=[sq_b])
                            else:
                                T.op("dve", lambda e: e.tensor_copy(out=sq[:, tt, :], in_=ps), reads=[ps_b], writes=[sq_b])
                            continue
                        gt = blk * TT + tt
                        xs, xs_b = xs_r.next()
                        T.op("act", lambda e: e.activation(out=xs[:], in_=ps, func=AF.Copy), reads=[ps_b], writes=[xs_b])
                        xv = xs[:].rearrange("p (h e) -> p h e", h=2)
                        x1 = xv[:, :, 0:16]
                        x2 = xv[:, :, 16:32]
                        cc = cs_sb[:, gt, 0, :, :]
                        ss = cs_sb[:, gt, 1, :, :]
                        tm, tm_b = tmp.next()
                        T.op("dve", lambda e: e.tensor_tensor(out=tm[:, 0], in0=x1, in1=cc, op=ALU.mult), reads=[xs_b, cs_b], writes=[tm_b])
                        T.op("dve", lambda e: e.tensor_tensor(out=tm[:, 1], in0=x2, in1=ss, op=ALU.mult), reads=[xs_b, cs_b], writes=[tm_b])
                        T.op("dve", lambda e: e.tensor_tensor(out=tm[:, 2], in0=x2, in1=cc, op=ALU.mult), reads=[xs_b, cs_b], writes=[tm_b])
                        T.op("dve", lambda e: e.tensor_tensor(out=tm[:, 3], in0=x1, in1=ss, op=ALU.mult), reads=[xs_b, cs_b], writes=[tm_b])
                        T.op("dve", lambda e: e.tensor_tensor(out=x1, in0=tm[:, 0], in1=tm[:, 1], op=ALU.subtract),
                             reads=[tm_b], writes=[xs_b])
                        T.op("dve", lambda e: e.tensor_tensor(out=x2, in0=tm[:, 2], in1=tm[:, 3], op=ALU.add),
                             reads=[tm_b], writes=[xs_b])
                        T.op("act", lambda e: e.activation(out=sq[:, tt, :], in_=xs[:], func=AF.Copy), reads=[xs_b], writes=[sq_b])
                    c0 = oc0 + ct * 256
                    ov = scr["qkv"][t0:t0 + TB, c0:c0 + 256].rearrange("(t p) c -> p t c", p=128)
                    T.dma("sp", ov, sq[:], reads=[sq_b], writes=[scr["qkv_b"]])
                tm_proj(nc, T, xnT, lambda k, tt: xnT_b[tt], NCH,
                        lambda k0, k1, c0, c1, wc0=wc0: w_in[k0 * 128:k1 * 128, wc0 + c0:wc0 + c1].rearrange("(k p) c -> p k c", p=128),
                        ncols, wr, ps_r, epi)
        T.barrier()


def lru_phase(nc, T, P, C, l, prm, scr):
    with contextlib.ExitStack() as es:
        NG = 8
        wa = es.enter_context(nc.sbuf_tensor(U("l_wa"), [128, NG, 128], BF16))
        wx = es.enter_context(nc.sbuf_tensor(U("l_wx"), [128, NG, 128], BF16))
        pv = es.enter_context(nc.sbuf_tensor(U("l_pv"), [128, 9, NG], F32))
        dv = es.enter_context(nc.sbuf_tensor(U("l_dv"), [128, 4, NG], F32))
        cb = Buf("lru_const")
        T.dma("pool", wa[:], prm["lru_w_a"][l].rearrange("g i j -> i g j"), writes=[cb])
        T.dma("pool", wx[:], prm["lru_w_x"][l].rearrange("g i j -> i g j"), writes=[cb])
        T.dma("sp", pv[:], prm["lru_pv"][l], writes=[cb])
        T.op("act", lambda e: e.activation(out=dv[:, 0, :], in_=pv[:, 7, :], func=AF.Exp, scale=-1.0), reads=[cb], writes=[cb])
        T.op("act", lambda e: e.activation(out=dv[:, 1, :], in_=dv[:, 0, :], func=AF.Ln, bias=C["one"][:, 0:1]), reads=[cb, C["cb"]], writes=[cb])
        T.op("dve", lambda e: e.tensor_scalar(out=dv[:, 2, :], in0=dv[:, 1, :], scalar1=-8.0, scalar2=None, op0=ALU.mult), reads=[cb], writes=[cb])
        T.op("dve", lambda e: e.tensor_scalar(out=dv[:, 3, :], in0=dv[:, 1, :], scalar1=-16.0, scalar2=None, op0=ALU.mult), reads=[cb], writes=[cb])
        xa_r = Ring(nc, es, "l_xa", [128, S + 4], F32, 2)
        for t_ in xa_r.t:
            T.op("dve", lambda e: e.memset(t_[:, 0:4], 0.0), writes=[cb])
        ga_r = Ring(nc, es, "l_ga", [128, S], F32, 2)
        xc_r = Ring(nc, es, "l_xc", [128, S], F32, 1)
        xcb_r = Ring(nc, es, "l_xcb", [128, S], BF16, 1)
        r_r = Ring(nc, es, "l_r", [128, S], F32, 1)
        i_r = Ring(nc, es, "l_i", [128, S], F32, 1)
        a_r = Ring(nc, es, "l_a", [128, S], F32, 1)
        m_r = Ring(nc, es, "l_m", [128, S], F32, 1)
        t_r = Ring(nc, es, "l_t", [128, S], F32, 1)
        y_r = Ring(nc, es, "l_y", [128, S], F32, 2)
        ps_r = P.ring(range(8))
        gn = GroupNorm(nc, T, P, C, es, D_LRU)
        for g in range(NG):
            xa, xa_b = xa_r.next()
            ga, ga_b = ga_r.next()
            xc, xc_b = xc_r.next()
            xcb, xcb_b = xcb_r.next()
            r, r_b = r_r.next()
            ii, i_b = i_r.next()
            a, a_b = a_r.next()
            m, m_b = m_r.next()
            t, t_b = t_r.next()
            y, y_b = y_r.next()
            T.dma("sp", xa[:, 3:3 + S], scr["xgT"][g * 128:(g + 1) * 128, :], reads=[scr["xgT_b"][g], cb], writes=[xa_b])
            T.dma("sp", ga[:], scr["xgT"][(8 + g) * 128:(9 + g) * 128, :], reads=[scr["xgT_b"][8 + g]], writes=[ga_b])
            T.op("dve", lambda e: e.tensor_scalar(out=xc[:], in0=xa[:, 0:S], scalar1=pv[:, 0, g:g + 1], scalar2=pv[:, 4, g:g + 1],
                                                  op0=ALU.mult, op1=ALU.add), reads=[xa_b, cb], writes=[xc_b])
            for j in range(1, 4):
                T.op("dve", lambda e: e.scalar_tensor_tensor(out=xc[:], in0=xa[:, j:j + S], scalar=pv[:, j, g:g + 1], in1=xc[:],
                                                             op0=ALU.mult, op1=ALU.add), reads=[xa_b, cb, xc_b], writes=[xc_b])
            T.op("act", lambda e: e.activation(out=xcb[:], in_=xc[:], func=AF.Copy), reads=[xc_b], writes=[xcb_b])
            for (wt, bcol, dst, dst_b) in ((wa, 5, r, r_b), (wx, 6, ii, i_b)):
                for q in range(4):
                    ps, ps_b = ps_r.next()
                    T.op("pe", lambda e: e.matmul(out=ps[:], lhsT=wt[:, g, :], rhs=xcb[:, q * 512:(q + 1) * 512], start=True, stop=True),
                         reads=[cb, xcb_b], writes=[ps_b])
                    T.op("act", lambda e: e.activation(out=dst[:, q * 512:(q + 1) * 512], in_=ps[:], func=AF.Sigmoid,
                                                       bias=pv[:, bcol, g:g + 1]), reads=[ps_b, cb], writes=[dst_b])
            T.op("act", lambda e: e.activation(out=a[:], in_=r[:], func=AF.Exp, scale=dv[:, 2, g:g + 1]), reads=[r_b, cb], writes=[a_b])
            T.op("act", lambda e: e.activation(out=m[:], in_=r[:], func=AF.Exp, scale=dv[:, 3, g:g + 1]), reads=[r_b, cb], writes=[m_b])
            T.op("dve", lambda e: e.tensor_scalar(out=m[:], in0=m[:], scalar1=-1.0, scalar2=1.0, op0=ALU.mult, op1=ALU.add),
                 reads=[m_b], writes=[m_b])
            T.op("act", lambda e: e.activation(out=m[:], in_=m[:], func=AF.Sqrt), reads=[m_b], writes=[m_b])
            T.op("dve", lambda e: e.tensor_tensor(out=m[:], in0=m[:], in1=ii[:], op=ALU.mult), reads=[m_b, i_b], writes=[m_b])
            T.op("dve", lambda e: e.tensor_tensor(out=m[:], in0=m[:], in1=xc[:], op=ALU.mult), reads=[m_b, xc_b], writes=[m_b])
            T.op("dve", lambda e: e.tensor_tensor_scan(out=r[:], data0=a[:], data1=m[:], initial=0.0, op0=ALU.mult, op1=ALU.add),
                 reads=[a_b, m_b], writes=[r_b])
            T.op("dve", lambda e: e.tensor_tensor(out=t[:], in0=ga[:], in1=ga[:], op=ALU.mult), reads=[ga_b], writes=[t_b])
            T.op("dve", lambda e: e.tensor_scalar(out=t[:], in0=t[:], scalar1=0.044715, scalar2=1.0, op0=ALU.mult, op1=ALU.add),
                 reads=[t_b], writes=[t_b])
            T.op("dve", lambda e: e.tensor_tensor(out=t[:], in0=t[:], in1=ga[:], op=ALU.mult), reads=[t_b, ga_b], writes=[t_b])
            T.op("act", lambda e: e.activation(out=t[:], in_=t[:], func=AF.Sigmoid, scale=1.5957691216057308), reads=[t_b], writes=[t_b])
            T.op("dve", lambda e: e.tensor_tensor(out=t[:], in0=t[:], in1=ga[:], op=ALU.mult), reads=[t_b, ga_b], writes=[t_b])
            T.op("dve", lambda e: e.tensor_tensor(out=y[:], in0=t[:], in1=r[:], op=ALU.mult), reads=[t_b, r_b], writes=[y_b])
            gn.add(y, y_b, scr, g, first=(g == 0))
        gn.finish(scr, 0, NG, prm["gn_gain"][l])
        T.barrier()


class GroupNorm:
    def __init__(self, nc, T, P, C, es, width):
        self.nc, self.T, self.P, self.C = nc, T, P, C
        self.width = width
        self.acc = es.enter_context(nc.sbuf_tensor(U("gn_acc"), [128, S], F32))
        self.acc_b = Buf("gn_acc")
        self.sq_r = Ring(nc, es, "gn_sq", [128, S], F32, 1)
        self.ld_r = Ring(nc, es, "gn_ld", [128, S], F32, 2)
        self.o_r = Ring(nc, es, "gn_o", [128, S], BF16, 2)
        self.gain = es.enter_context(nc.sbuf_tensor(U("gn_gain"), [128, NCH], F32))
        self.gain_b = Buf("gn_gain")
        self.ps_r = P.ring([6, 7])

    def add(self, y, y_b, scr, chunk, first):
        T, C = self.T, self.C
        T.dma("sp", scr["yraw"][chunk * 128:(chunk + 1) * 128, :], y[:], reads=[y_b], writes=[scr["yraw_b"][chunk]])
        sq, sq_b = self.sq_r.next()
        T.op("act", lambda e: e.activation(out=sq[:], in_=y[:], func=AF.Square), reads=[y_b], writes=[sq_b])
        for q in range(4):
            ps, ps_b = self.ps_r.next()
            T.op("pe", lambda e: e.matmul(out=ps[:], lhsT=C["ones_f"][:], rhs=sq[:, q * 512:(q + 1) * 512], start=True, stop=True),
                 reads=[C["cb"], sq_b], writes=[ps_b])
            if first:
                T.op("dve", lambda e: e.tensor_copy(out=self.acc[:, q * 512:(q + 1) * 512], in_=ps[:]), reads=[ps_b], writes=[self.acc_b])
            else:
                T.op("dve", lambda e: e.tensor_tensor(out=self.acc[:, q * 512:(q + 1) * 512], in0=self.acc[:, q * 512:(q + 1) * 512],
                                                      in1=ps[:], op=ALU.add), reads=[ps_b, self.acc_b], writes=[self.acc_b])

    def finish(self, scr, chunk0, nchunks, gain_pc):
        T, C = self.T, self.C
        T.dma("sp", self.gain[:], gain_pc, writes=[self.gain_b])
        T.op("act", lambda e: e.activation(out=self.acc[:], in_=self.acc[:], func=AF.Sqrt, scale=1.0 / self.width, bias=C["eps"][:, 0:1]),
             reads=[self.acc_b, C["cb"]], writes=[self.acc_b])
        T.op("dve", lambda e: e.reciprocal(out=self.acc[:], in_=self.acc[:]), reads=[self.acc_b], writes=[self.acc_b])
        for c in range(chunk0, chunk0 + nchunks):
            ld, ld_b = self.ld_r.next()
            o, o_b = self.o_r.next()
            T.dma("sp", ld[:], scr["yraw"][c * 128:(c + 1) * 128, :], reads=[scr["yraw_b"][c]], writes=[ld_b])
            T.op("dve", lambda e: e.scalar_tensor_tensor(out=o[:], in0=ld[:], scalar=self.gain[:, c:c + 1], in1=self.acc[:],
                                                         op0=ALU.mult, op1=ALU.mult), reads=[ld_b, self.gain_b, self.acc_b], writes=[o_b])
            T.dma("sp", scr["yT"][c * 128:(c + 1) * 128, :], o[:], reads=[o_b], writes=[scr["yT_b"]])


def attn_phase(nc, T, P, C, l, prm, scr, fox):
    scale = HD ** -0.5
    nh = N_FOX if fox else N_DIL
    qc0, kc0, vc0 = (0, 1024, 2048) if fox else (3072, 5120, 7168)
    chunk0 = 8 if fox else 16
    with contextlib.ExitStack() as es:
        cb = Buf("attn_const")
        mask = es.enter_context(nc.sbuf_tensor(U("a_mask"), [128, S if not fox else 128], F32))
        T.dma("sp", mask[:], C["mask_causal"] if fox else C["mask_dil"], writes=[cb])
        if fox:
            ff = es.enter_context(nc.sbuf_tensor(U("a_ff"), [128, S], F32))
            cum = es.enter_context(nc.sbuf_tensor(U("a_cum"), [128, S], F32))
            onesr = es.enter_context(nc.sbuf_tensor(U("a_onesr"), [128, S], F32))
            fb = es.enter_context(nc.sbuf_tensor(U("a_fb"), [8, 2], F32))
            sel = es.enter_context(nc.sbuf_tensor(U("a_sel"), [128, 8, 128], F32))
            T.op("dve", lambda e: e.memset(cum[:], 0.0), writes=[cb])
            T.dma("sp", ff[0:8, :], scr["ffT"], reads=[scr["ffT_b"]], writes=[cb])
            T.dma("sp", fb[:, 0:1], prm["fox_b_f"][l], writes=[cb])
            T.dma("sp", sel[:], C["sel"], writes=[cb])
            T.op("dve", lambda e: e.memset(onesr[:], 1.0), writes=[cb])
            T.op("dve", lambda e: e.tensor_scalar(out=fb[:, 1:2], in0=fb[:, 0:1], scalar1=-1.0, scalar2=None, op0=ALU.mult), reads=[cb], writes=[cb])
            T.op("act", lambda e: e.activation(out=ff[0:8, :], in_=ff[0:8, :], func=AF.Exp, scale=-1.0, bias=fb[:, 1:2]), reads=[cb], writes=[cb])
            T.op("act", lambda e: e.activation(out=ff[0:8, :], in_=ff[0:8, :], func=AF.Ln, bias=C["one"][0:8, 0:1]), reads=[cb, C["cb"]], writes=[cb])
            T.op("dve", lambda e: e.tensor_tensor_scan(out=cum[0:8, :], data0=onesr[0:8, :], data1=ff[0:8, :], initial=0.0, op0=ALU.mult, op1=ALU.add),
                 reads=[cb], writes=[cb])
            nd_r = Ring(nc, es, "a_nd", [128, S], F32, 2)
        qk_r = Ring(nc, es, "a_qk", [128, 2, S // 128, 128], BF16, 2)
        v_r = Ring(nc, es, "a_v", [128, S // 128, 128], BF16, 2)
        qkT_r = Ring(nc, es, "a_qkT", [128, 2, S], BF16, 2)
        z_r = Ring(nc, es, "a_z", [128, S], F32, 2)
        p_r = Ring(nc, es, "a_p", [128, S], BF16, 3)
        pT_r = Ring(nc, es, "a_pT", [128, S // 128, 128], BF16, 2)
        st_r = Ring(nc, es, "a_st", [128, 4], F32, 5)
        dg_r = Ring(nc, es, "a_dg", [128, 128], BF16, 2)
        y_r = Ring(nc, es, "a_y", [128, S], F32, 2)
        s_ps = P.ring([0, 1, 2, 3])
        t_ps = P.ring([4, 5])
        o_ps = P.ring([6, 7])
        gn = GroupNorm(nc, T, P, C, es, D_FOX if fox else D_DIL)

        def head_setup(hh):
            qk, qk_b = qk_r.next()
            v, v_b = v_r.next()
            qkT, qkT_b = qkT_r.next()
            y, y_b = y_r.next()
            for j, c0 in enumerate((qc0, kc0)):
                T.dma("sp", qk[:, j], scr["qkv"][:, c0 + hh * 128:c0 + (hh + 1) * 128].rearrange("(t p) e -> p t e", p=128),
                      reads=[scr["qkv_b"]], writes=[qk_b])
            T.dma("sp", v[:], scr["qkv"][:, vc0 + hh * 128:vc0 + (hh + 1) * 128].rearrange("(t p) e -> p t e", p=128),
                  reads=[scr["qkv_b"]], writes=[v_b])
            for j in range(2):
                for t8 in range(2):
                    psf, ps_b = s_ps.next()
                    tp = psf[:].bitcast(BF16).rearrange("p (j c) -> p j c", c=128)
                    for i in range(8):
                        T.op("pe", lambda e: e.transpose(out=tp[:, i, :], in_=qk[:, j, t8 * 8 + i, :], identity=C["ident_bf"][:]),
                             reads=[qk_b, C["cb"]], writes=[ps_b])
                    if t8 == 0:
                        T.op("act", lambda e: e.activation(out=qkT[:, j, t8 * 1024:(t8 + 1) * 1024], in_=psf[:].bitcast(BF16), func=AF.Copy),
                             reads=[ps_b], writes=[qkT_b])
                    else:
                        T.op("dve", lambda e: e.tensor_copy(out=qkT[:, j, t8 * 1024:(t8 + 1) * 1024], in_=psf[:].bitcast(BF16)),
                             reads=[ps_b], writes=[qkT_b])
            nd = nd_b = None
            if fox:
                nd, nd_b = nd_r.next()
                for q in range(4):
                    ps, ps_b = s_ps.next()
                    T.op("pe", lambda e: e.matmul(out=ps[:], lhsT=sel[:, hh, :], rhs=cum[:, q * 512:(q + 1) * 512], start=True, stop=True),
                         reads=[cb], writes=[ps_b])
                    T.op("act", lambda e: e.activation(out=nd[:, q * 512:(q + 1) * 512], in_=ps[:], func=AF.Copy), reads=[ps_b], writes=[nd_b])
            return dict(hh=hh, v=v, v_b=v_b, qkT=qkT, qkT_b=qkT_b, y=y, y_b=y_b, nd=nd, nd_b=nd_b)

        def stage_1(H, qb):
            nk = (qb + 1) * 128
            z, z_b = z_r.next()
            p, p_b = p_r.next()
            st, st_b = st_r.next()
            qkT, qkT_b = H["qkT"], H["qkT_b"]
            for c in range((nk + 511) // 512):
                w = min(512, nk - c * 512)
                ps, ps_b = s_ps.next()
                T.op("pe", lambda e: e.matmul(out=ps[:, 0:w], lhsT=qkT[:, 0, qb * 128:(qb + 1) * 128], rhs=qkT[:, 1, c * 512:c * 512 + w],
                                              start=True, stop=True), reads=[qkT_b], writes=[ps_b])
                if fox:
                    T.op("dve", lambda e: e.scalar_tensor_tensor(out=z[:, c * 512:c * 512 + w], in0=ps[:, 0:w], scalar=scale,
                                                                 in1=H["nd"][:, c * 512:c * 512 + w], op0=ALU.mult, op1=ALU.add),
                         reads=[ps_b, H["nd_b"]], writes=[z_b])
                else:
                    m0 = (15 - qb) * 128 + c * 512
                    T.op("dve", lambda e: e.scalar_tensor_tensor(out=z[:, c * 512:c * 512 + w], in0=ps[:, 0:w], scalar=scale,
                                                                 in1=mask[:, m0:m0 + w], op0=ALU.mult, op1=ALU.add),
                         reads=[ps_b, cb], writes=[z_b])
            if fox:
                T.op("dve", lambda e: e.tensor_tensor(out=z[:, nk - 128:nk], in0=z[:, nk - 128:nk], in1=mask[:], op=ALU.add),
                     reads=[z_b, cb], writes=[z_b])
            T.op("dve", lambda e: e.reduce_max(out=st[:, 0:1], in_=z[:, 0:nk], axis=AX.X), reads=[z_b], writes=[st_b])
            T.op("dve", lambda e: e.tensor_scalar(out=st[:, 1:2], in0=st[:, 0:1], scalar1=-1.0, scalar2=None, op0=ALU.mult),
                 reads=[st_b], writes=[st_b])
            return dict(H=H, qb=qb, nk=nk, z=z, z_b=z_b, p=p, p_b=p_b, st=st, st_b=st_b)

        def stage_2(I):
            nk, z, z_b, p, p_b, st, st_b = I["nk"], I["z"], I["z_b"], I["p"], I["p_b"], I["st"], I["st_b"]
            T.op("act", lambda e: e.activation(out=p[:, 0:nk], in_=z[:, 0:nk], func=AF.Exp, bias=st[:, 1:2], accum_out=st[:, 2:3]),
                 reads=[z_b, st_b], writes=[p_b, st_b])

        def stage_3a(I):
            qb, p, p_b, st, st_b = I["qb"], I["p"], I["p_b"], I["st"], I["st_b"]
            pT, pT_b = pT_r.next()
            dg, dg_b = dg_r.next()
            I["pT"], I["pT_b"] = pT, pT_b
            T.op("dve", lambda e: e.reciprocal(out=st[:, 3:4], in_=st[:, 2:3]), reads=[st_b], writes=[st_b])
            T.op("dve", lambda e: e.tensor_scalar(out=dg[:], in0=C["ident_f"][:], scalar1=st[:, 3:4], scalar2=None, op0=ALU.mult),
                 reads=[st_b, C["cb"]], writes=[dg_b])
            for k4 in range((qb + 4) // 4):
                nb = min(4, qb + 1 - k4 * 4)
                ps, ps_b = t_ps.next()
                for i in range(nb):
                    kb = k4 * 4 + i
                    T.op("pe", lambda e: e.matmul(out=ps[:, i * 128:(i + 1) * 128], lhsT=p[:, kb * 128:(kb + 1) * 128], rhs=dg[:],
                                                  start=True, stop=True), reads=[p_b, dg_b], writes=[ps_b])
                T.op("act", lambda e: e.activation(out=pT[:, k4 * 4:k4 * 4 + nb, :], in_=ps[:, 0:nb * 128].rearrange("p (i c) -> p i c", c=128),
                                                   func=AF.Copy), reads=[ps_b], writes=[pT_b])

        def stage_3b(I):
            H, qb, pT, pT_b = I["H"], I["qb"], I["pT"], I["pT_b"]
            po, po_b = o_ps.next()
            v, v_b = H["v"], H["v_b"]
            for kb in range(qb + 1):
                T.op("pe", lambda e: e.matmul(out=po[:, 0:128], lhsT=v[:, kb, :], rhs=pT[:, kb, :], start=(kb == 0), stop=(kb == qb)),
                     reads=[v_b, pT_b], writes=[po_b])
            T.op("act", lambda e: e.activation(out=H["y"][:, qb * 128:(qb + 1) * 128], in_=po[:, 0:128], func=AF.Copy),
                 reads=[po_b], writes=[H["y_b"]])
            if qb == S // 128 - 1:
                gn.add(H["y"], H["y_b"], scr, chunk0 + H["hh"], first=(H["hh"] == 0))

        items = [(hh, qb) for hh in range(nh) for qb in range(S // 128)]
        n = len(items)
        live = {}
        Hcur = None
        for step in range(n + 2):
            k3, k1, k2 = step - 2, step, step - 1
            if k3 >= 0:
                stage_3a(live[k3])
            if k1 < n:
                hh, qb = items[k1]
                if qb == 0:
                    Hcur = head_setup(hh)
                live[k1] = stage_1(Hcur, qb)
            if 0 <= k2 < n:
                stage_2(live[k2])
            if k3 >= 0:
                stage_3b(live[k3])
                del live[k3]
        gn.finish(scr, chunk0, nh, prm["gn_gain"][l])
        T.barrier()


def outproj_phase(nc, T, P, C, l, hsrc, hdst, hb, prm, scr):
    w_out = prm["mix_w_out"][l]
    with contextlib.ExitStack() as es:
        yT = es.enter_context(nc.sbuf_tensor(U("o_yT"), [128, NCH, TB], BF16))
        yT_b = Buf("o_yT")
        wr = Ring(nc, es, "o_w", [128, 8, 256], BF16, 4)
        hp_r = Ring(nc, es, "o_hp", [128, TT, 256], F32, 2)
        ps_r = P.ring(range(8))
        for blk in range(NTB):
            t0 = blk * TB
            T.dma("sp", yT[:], scr["yT"][:, t0:t0 + TB].rearrange("(c p) t -> p c t", p=128), reads=[scr["yT_b"]], writes=[yT_b])
            tm_proj_residual(nc, T, yT, lambda k, tt: yT_b, NCH,
                             lambda k0, k1, c0, c1: w_out[k0 * 128:k1 * 128, c0:c1].rearrange("(k p) c -> p k c", p=128),
                             hsrc, hdst, hb[blk], blk, 1.0, wr, ps_r, hp_r)
        T.barrier()


def final_norm_phase(nc, T, P, C, h, hb, out, g_row_bc):
    with contextlib.ExitStack() as es:
        g = es.enter_context(nc.sbuf_tensor(U("fn_g"), [128, D], F32))
        g_b = Buf("fn_g")
        T.dma("sp", g[:], g_row_bc, writes=[g_b])
        hs_r = Ring(nc, es, "fn_hs", [128, D], F32, 2)
        o_r = Ring(nc, es, "fn_o", [128, D], F32, 2)
        st_r = Ring(nc, es, "fn_st", [128, 4], F32, 2)
        ob = Buf("out")
        for tt in range(S // 128):
            hs, hs_b = hs_r.next()
            o, o_b = o_r.next()
            st, st_b = st_r.next()
            T.dma("sp", hs[:], h[tt * 128:(tt + 1) * 128, :], reads=hb[tt // TT], writes=[hs_b])
            T.op("act", lambda e: e.activation(out=o[:], in_=hs[:], func=AF.Square, accum_out=st[:, 0:1]), reads=[hs_b], writes=[o_b, st_b])
            T.op("act", lambda e: e.activation(out=st[:, 1:2], in_=st[:, 0:1], func=AF.Sqrt, scale=1.0 / D, bias=C["eps"][:, 0:1]),
                 reads=[st_b, C["cb"]], writes=[st_b])
            T.op("dve", lambda e: e.reciprocal(out=st[:, 2:3], in_=st[:, 1:2]), reads=[st_b], writes=[st_b])
            T.op("dve", lambda e: e.scalar_tensor_tensor(out=o[:], in0=hs[:], scalar=st[:, 2:3], in1=g[:], op0=ALU.mult, op1=ALU.mult),
                 reads=[hs_b, st_b, g_b], writes=[o_b])
            T.dma("sp", out[tt * 128:(tt + 1) * 128, :], o[:], reads=[o_b], writes=[ob])
        T.barrier()


PARAMS = (("ffn1_norm", [DEPTH, 128, NCH]), ("ffn1_w_in", [DEPTH, D, 2 * DFF]), ("ffn1_w_out", [DEPTH, DFF, D]),
          ("mix_norm", [DEPTH, 128, NCH]), ("mix_w_in", [DEPTH, D, D_IN]), ("lru_w_a", [DEPTH, 8, 128, 128]),
          ("lru_w_x", [DEPTH, 8, 128, 128]), ("lru_pv", [DEPTH, 128, 9, 8]), ("fox_b_f", [DEPTH, 8, 1]),
          ("gn_gain", [DEPTH, 128, NCH]), ("mix_w_out", [DEPTH, D, D]),
          ("ffn2_norm", [DEPTH, 128, NCH]), ("ffn2_w_in", [DEPTH, D, 2 * DFF]), ("ffn2_w_out", [DEPTH, DFF, D]),
          ("final_norm", [128, D]))
CONSTS = (("ident", [128, 128]), ("rope", [128, S // 128, 2, 2, 16]), ("mask_dil", [128, S]), ("mask_causal", [128, 128]),
          ("sel", [128, 8, 128]))


def build_program(phases="all", debug=False):
    nc = bass.Bass("TRN2", target_bir_lowering=False)
    x = nc.dram_tensor("x", [S, D], F32, kind="ExternalInput").ap()
    prm = {}
    for name, shape in PARAMS:
        prm[name] = nc.dram_tensor(name, shape, F32, kind="ExternalInput").ap()
    cin = {}
    for name, shape in CONSTS:
        cin[name] = nc.dram_tensor(name, shape, F32, kind="ExternalInput").ap()
    out = nc.dram_tensor("out", [S, D], F32, kind="ExternalOutput").ap()
    h = nc.dram_tensor("h_scr", [S, D], F32, kind="Internal").ap()
    hb = [[Buf(f"h{b}_{c}") for c in range(D // 256)] for b in range(NTB)]
    sk = "ExternalOutput" if debug else "Internal"
    sk2 = "ExternalOutput" if debug == 2 else "Internal"
    scr = {
        "xgT": nc.dram_tensor("s_xgT", [2048, S], F32, kind=sk).ap(), "xgT_b": [Buf() for _ in range(16)],
        "ffT": nc.dram_tensor("s_ffT", [8, S], F32, kind=sk).ap(), "ffT_b": Buf(),
        "qkv": nc.dram_tensor("s_qkv", [S, 9216], BF16, kind=sk2).ap(), "qkv_b": Buf(),
        "yraw": nc.dram_tensor("s_yraw", [D, S], F32, kind=sk).ap(), "yraw_b": [Buf() for _ in range(NCH)],
        "yT": nc.dram_tensor("s_yT", [D, S], BF16, kind=sk2).ap(), "yT_b": Buf(),
    }

    with contextlib.ExitStack() as es:
        T = TK(nc, es)
        P = PS(nc, es)
        C = {"cb": Buf("const")}
        cst = C["cb"]
        C["ident_f"] = es.enter_context(nc.sbuf_tensor(U("ident_f"), [128, 128], F32))
        C["ident_bf"] = es.enter_context(nc.sbuf_tensor(U("ident_bf"), [128, 128], BF16))
        C["ones_f"] = es.enter_context(nc.sbuf_tensor(U("ones_f"), [128, 128], F32))
        C["eps"] = es.enter_context(nc.sbuf_tensor(U("eps_sb"), [128, 1], F32))
        C["one"] = es.enter_context(nc.sbuf_tensor(U("one_sb"), [128, 1], F32))
        for k in ("rope", "mask_dil", "mask_causal", "sel"):
            C[k] = cin[k]
        T.dma("sp", C["ident_f"][:], cin["ident"], writes=[cst])
        T.op("dve", lambda e: e.tensor_copy(out=C["ident_bf"][:], in_=C["ident_f"][:]), reads=[cst], writes=[cst])
        T.op("dve", lambda e: e.memset(C["ones_f"][:], 1.0), writes=[cst])
        T.op("dve", lambda e: e.memset(C["eps"][:], EPS), writes=[cst])
        T.op("dve", lambda e: e.memset(C["one"][:], 1.0), writes=[cst])

        def mixer(l, hsrc, hdst, sub=(1, 2, 3, 4, 5)):
            if 1 in sub:
                mixer_proj_phase(nc, T, P, C, l, hsrc, hb, prm, scr)
            if 2 in sub:
                lru_phase(nc, T, P, C, l, prm, scr)
            if 3 in sub:
                attn_phase(nc, T, P, C, l, prm, scr, fox=True)
            if 4 in sub:
                attn_phase(nc, T, P, C, l, prm, scr, fox=False)
            if 5 in sub:
                outproj_phase(nc, T, P, C, l, hsrc, hdst, hb, prm, scr)

        if phases == "ffn1":
            ffn_phase(nc, T, P, C, x, out, hb, prm["ffn1_norm"][0], prm["ffn1_w_in"][0], prm["ffn1_w_out"][0])
        elif phases.startswith("mix"):
            mixer(0, x, out, sub=tuple(int(c) for c in phases[3:]) if len(phases) > 3 else (1, 2, 3, 4, 5))
        elif phases == "all":
            src = x
            for l in range(DEPTH):
                ffn_phase(nc, T, P, C, src, h, hb, prm["ffn1_norm"][l], prm["ffn1_w_in"][l], prm["ffn1_w_out"][l])
                src = h
                mixer(l, h, h)
                ffn_phase(nc, T, P, C, h, h, hb, prm["ffn2_norm"][l], prm["ffn2_w_in"][l], prm["ffn2_w_out"][l])
            final_norm_phase(nc, T, P, C, h, hb, out, prm["final_norm"])
        T.barrier()
        print("instructions", T.ninst, "waits", T.nwaits)
    return nc


def pc_layout(v):
    sh = v.shape
    return np.ascontiguousarray(v.reshape(sh[:-1] + (sh[-1] // 128, 128)).swapaxes(-1, -2))


def make_consts():
    c = {"ident": np.eye(128, dtype=np.float32)}
    inv = (1.0 / (np.float32(500000.0) ** (np.arange(0, 32, 2, dtype=np.float32) / np.float32(32)))).astype(np.float32)
    pos = np.arange(S, dtype=np.float32)
    ang = (pos[:, None] * inv[None, :]).astype(np.float32)
    cs = np.stack([np.cos(ang), np.sin(ang)], axis=1).astype(np.float32)
    cs = cs.reshape(S // 128, 128, 2, 1, 16).transpose(1, 0, 2, 3, 4)
    c["rope"] = np.ascontiguousarray(np.broadcast_to(cs, (128, S // 128, 2, 2, 16))).astype(np.float32)
    p = np.arange(128)[:, None]
    u = np.arange(S)[None, :]
    rel = p + (S - 128) - u
    cnt = ((rel >= 0) & (rel <= 128)).astype(np.float32) + ((rel >= 0) & (rel % 4 == 0) & (rel <= 512)) \
        + ((rel >= 0) & (rel % 16 == 0) & (rel <= 2048))
    c["mask_dil"] = np.where(cnt > 0, np.log(np.maximum(cnt, 1.0)), NEG).astype(np.float32)
    j = np.arange(128)[None, :]
    c["mask_causal"] = np.where(j <= p, 0.0, NEG).astype(np.float32)
    sel = np.zeros((128, 8, 128), np.float32)
    for i in range(8):
        sel[i, i, :] = 1.0
    c["sel"] = sel
    return c


def layout_params(inp):
    f = lambda k: np.ascontiguousarray(np.asarray(inp[k], dtype=np.float32))
    o = {}
    for k in ("ffn1_w_in", "ffn1_w_out", "mix_w_in", "lru_w_a", "lru_w_x", "mix_w_out", "ffn2_w_in", "ffn2_w_out"):
        o[k] = f(k)
    for k in ("ffn1_norm", "mix_norm", "ffn2_norm"):
        o[k] = pc_layout(f(k))
    cw = f("conv_w")
    rows = [cw[:, j] for j in range(4)] + [f("conv_b"), f("lru_b_a"), f("lru_b_x"), f("lru_lam"), f("out_norm_lru")]
    pv = np.stack(rows, axis=1)
    o["lru_pv"] = np.ascontiguousarray(pv.reshape(DEPTH, 9, 8, 128).transpose(0, 3, 1, 2))
    o["fox_b_f"] = f("fox_b_f").reshape(DEPTH, 8, 1)
    o["gn_gain"] = pc_layout(np.concatenate([f("out_norm_lru"), f("out_norm_fox"), f("out_norm_dil")], axis=-1))
    o["final_norm"] = np.ascontiguousarray(np.broadcast_to(f("final_norm")[None, :], (128, D)))
    return o


_NC_CACHE = {}


def kernel(**inputs):
    n = 8
    if "all" not in _NC_CACHE:
        _NC_CACHE["all"] = build_program("all")
    nc = _NC_CACHE["all"]
    shared = layout_params(inputs)
    shared.update(make_consts())
    x = np.asarray(inputs["x"], dtype=np.float32)
    in_maps = []
    for b in range(n):
        m = dict(shared)
        m["x"] = np.ascontiguousarray(x[b])
        in_maps.append(m)
    res = run_bass_kernel_spmd(nc, in_maps, core_ids=list(range(n)))
    return np.stack([np.asarray(r["out"]) for r in res.results], axis=0).astype(np.float32)
```

```python
import contextlib
import os
_DBG = os.environ.get("MK_DBG", "")
import numpy as np
import concourse.bass as bass
import concourse.mybir as mybir
from concourse.bass_utils import run_bass_kernel_spmd

F32 = mybir.dt.float32
BF16 = mybir.dt.bfloat16
AF = mybir.ActivationFunctionType
ALU = mybir.AluOpType
AX = mybir.AxisListType

S = 2048
D = 4096
DFF = 11008
DEPTH = 2
HD = 128
D_LRU = 1024
D_FOX = 1024
N_FOX = 8
D_DIL = 2048
N_DIL = 16
D_IN = 11272
EPS = 1e-6
NCH = D // 128
FCH = DFF // 128
TB = 1024
NTB = S // TB
TT = TB // 128
NEG = -30000.0


_UID = [0]


def U(name):
    _UID[0] += 1
    return f"{name}_{_UID[0]}"


class Buf:
    __slots__ = ("name", "w", "r")

    def __init__(self, name=""):
        self.name = name
        self.w = None
        self.r = {}


class TK:
    def __init__(self, nc, es, n_sp=24, n_pool=12):
        self.nc = nc
        self.eng = {"pe": nc.tensor, "act": nc.scalar, "dve": nc.vector, "pool": nc.gpsimd, "sp": nc.sync}
        self.prog = {}
        for e in ("pe", "act", "dve", "pool"):
            self.prog[e] = [es.enter_context(nc.semaphore("pg_" + e)), 0]
        self.waited = {e: {} for e in self.eng}
        self.dpool = {
            "sp": [[es.enter_context(nc.semaphore(f"dsp{i}")), 0] for i in range(n_sp)],
            "pool": [[es.enter_context(nc.semaphore(f"dpl{i}")), 0] for i in range(n_pool)],
        }
        self.drr = {"sp": 0, "pool": 0}
        self.nwaits = 0
        self.ninst = 0

    def _wait(self, eng, sem, val):
        key = id(sem)
        w = self.waited[eng]
        if w.get(key, 0) >= val:
            return
        w[key] = val
        self.eng[eng].wait_ge(sem[0], val)
        self.nwaits += 1

    def _deps(self, eng, reads, writes):
        for b in reads:
            if b.w is not None:
                self._dep1(eng, b.w)
        for b in writes:
            if b.w is not None:
                self._dep1(eng, b.w)
            for t in b.r.values():
                self._dep1(eng, t)

    def _dep1(self, eng, tok):
        sem, val, peng = tok
        if peng == "pe" and eng == "pe":
            return
        self._wait(eng, sem, val)

    def _reg(self, tok, reads, writes):
        key = tok[2] if tok[2] != "dma" else id(tok[0])
        for b in writes:
            b.w = tok
            b.r = {}
        for b in reads:
            b.r[key] = tok

    def op(self, eng, inst_fn, reads=(), writes=()):
        self._deps(eng, reads, writes)
        inst = inst_fn(self.eng[eng])
        p = self.prog[eng]
        p[1] += 1
        inst.then_inc(p[0], 1)
        self.ninst += 1
        self._reg((p, p[1], eng), reads, writes)
        return inst

    def dma(self, q, out, in_, reads=(), writes=(), **kw):
        self._deps(q, reads, writes)
        pool = self.dpool[q]
        i = self.drr[q]
        self.drr[q] = (i + 1) % len(pool)
        s = pool[i]
        if s[1] > 0:
            self._wait(q, s, s[1])
        inst = self.eng[q].dma_start(out=out, in_=in_, **kw)
        s[1] += 16
        inst.then_inc(s[0], 16)
        self.ninst += 1
        self._reg((s, s[1], "dma"), reads, writes)
        return inst

    def barrier(self, engs=("pe", "act", "dve", "pool", "sp")):
        for e in engs:
            for e2, p in self.prog.items():
                if e2 != e and p[1] > 0:
                    self._wait(e, p, p[1])
            for q in self.dpool.values():
                for s in q:
                    if s[1] > 0:
                        self._wait(e, s, s[1])


class Ring:
    def __init__(self, nc, es, name, shape, dtype, n):
        self.t = [es.enter_context(nc.sbuf_tensor(U(f"{name}{i}"), shape, dtype)) for i in range(n)]
        self.b = [Buf(f"{name}{i}") for i in range(n)]
        self.i = 0
        self.n = n

    def next(self):
        i = self.i
        self.i = (i + 1) % self.n
        return self.t[i], self.b[i]


class PS:
    def __init__(self, nc, es):
        self.t = [es.enter_context(nc.psum_tensor(f"psb{i}", [128, 512], F32)) for i in range(8)]
        self.b = [Buf(f"psb{i}") for i in range(8)]

    def ring(self, idx):
        return PRing(self, idx)


class PRing:
    def __init__(self, ps, idx):
        self.ps = ps
        self.idx = list(idx)
        self.i = 0

    def next(self):
        j = self.idx[self.i]
        self.i = (self.i + 1) % len(self.idx)
        return self.ps.t[j], self.ps.b[j]


def norm_transpose_block(nc, T, es, P, hsrc, hbufs_blk, blk, g_sb, xnT, xnT_b, ident_bf, cst, eps_sb, CB):
    hs_r = Ring(nc, es, "nt_hs", [128, D], F32, 2)
    xn_r = Ring(nc, es, "nt_xn", [128, D], BF16, 2)
    st_r = Ring(nc, es, "nt_st", [128, 4], F32, 2)
    tp_r = P.ring([0, 1, 2, 3])
    for tt in range(TT):
        r0 = blk * TB + tt * 128
        hs, hs_b = hs_r.next()
        xn, xn_b = xn_r.next()
        st, st_b = st_r.next()
        T.dma("sp", hs[:], hsrc[r0:r0 + 128, :], reads=hbufs_blk, writes=[hs_b])
        T.op("act", lambda e: e.activation(out=xn[:], in_=hs[:], func=AF.Square, accum_out=st[:, 0:1]),
             reads=[hs_b], writes=[xn_b, st_b])
        T.op("act", lambda e: e.activation(out=st[:, 1:2], in_=st[:, 0:1], func=AF.Sqrt, scale=1.0 / D, bias=eps_sb[:, 0:1]),
             reads=[st_b, CB], writes=[st_b])
        T.op("dve", lambda e: e.reciprocal(out=st[:, 2:3], in_=st[:, 1:2]), reads=[st_b], writes=[st_b])
        T.op("dve", lambda e: e.tensor_scalar(out=xn[:], in0=hs[:], scalar1=st[:, 2:3], scalar2=None,
                                              op0=ALU.mult), reads=[hs_b, st_b], writes=[xn_b])
        for cg in range(NCH // 8):
            tpf, tp_b = tp_r.next()
            tp = tpf[:].bitcast(BF16).rearrange("p (j c) -> p j c", c=128)
            for j in range(8):
                c = cg * 8 + j
                T.op("pe", lambda e: e.transpose(out=tp[:, j, :], in_=xn[:, c * 128:(c + 1) * 128],
                                                 identity=ident_bf[:]), reads=[xn_b, CB], writes=[tp_b])
            for j in range(8):
                c = cg * 8 + j
                eng = "act" if j % 2 == 0 else "dve"
                if eng == "act":
                    T.op("act", lambda e: e.activation(out=xnT[:, c, tt * 128:(tt + 1) * 128], in_=tp[:, j, :],
                                                       func=AF.Copy, scale=g_sb[:, c:c + 1]),
                         reads=[tp_b, cst], writes=[xnT_b[tt]])
                else:
                    T.op("dve", lambda e: e.tensor_scalar(out=xnT[:, c, tt * 128:(tt + 1) * 128], in0=tp[:, j, :],
                                                          scalar1=g_sb[:, c:c + 1], scalar2=None, op0=ALU.mult),
                         reads=[tp_b, cst], writes=[xnT_b[tt]])


def tm_proj_residual(nc, T, lhsT, rb, nk, w_rows, hsrc, hdst, hb_blk, blk, alpha, wr, ps_r, hp_r):
    r0 = blk * TB

    def epi(ct, get):
        c0 = ct * 256
        hp, hp_b = hp_r.next()
        hv = hsrc[r0:r0 + TB, c0:c0 + 256].rearrange("(t p) c -> p t c", p=128)
        ov = hdst[r0:r0 + TB, c0:c0 + 256].rearrange("(t p) c -> p t c", p=128)
        cb = hb_blk[ct]
        T.dma("sp", hp[:], hv, reads=[cb], writes=[hp_b])
        for tt in range(TT):
            ps, ps_b = get(tt)
            T.op("dve", lambda e: e.scalar_tensor_tensor(out=hp[:, tt, :], in0=ps, scalar=alpha, in1=hp[:, tt, :],
                                                         op0=ALU.mult, op1=ALU.add), reads=[ps_b, hp_b], writes=[hp_b])
        T.dma("sp", ov, hp[:], reads=[hp_b], writes=[cb])
    tm_proj(nc, T, lhsT, rb, nk, w_rows, D, wr, ps_r, epi)


def ffn_phase(nc, T, P, C, hsrc, hdst, hb, g_pc, w_in, w_out, nblk=NTB, parts=None, dbg=None):
    ident_bf, cst, eps_sb = C["ident_bf"], C["cb"], C["eps"]
    if parts is None:
        parts = [(0, 22), (22, 44), (44, 65), (65, 86)]
    with contextlib.ExitStack() as es:
        xnT = es.enter_context(nc.sbuf_tensor(U("f_xnT"), [128, NCH, TB], BF16))
        xnT_b = [Buf(f"xnT{t}") for t in range(TT)]
        maxp = max(b - a for a, b in parts)
        actT = es.enter_context(nc.sbuf_tensor(U("f_actT"), [128, maxp, TB], BF16))
        actT_b = [Buf(f"actT{i}") for i in range(maxp)]
        g_sb = es.enter_context(nc.sbuf_tensor(U("f_g"), [128, NCH], F32))
        g_b = Buf("g")
        T.dma("sp", g_sb[:], g_pc, writes=[g_b])
        wr = Ring(nc, es, "f_w", [128, 8, 256], BF16, 4)
        hp_r = Ring(nc, es, "f_hp", [128, TT, 256], F32, 2)
        sg_r = Ring(nc, es, "f_sg", [128, 512], F32, 3)
        for blk in range(nblk):
            with contextlib.ExitStack() as es2:
                norm_transpose_block(nc, T, es2, P, hsrc, hb[blk], blk, g_sb, xnT, xnT_b, ident_bf, g_b, eps_sb, cst)
                T.barrier()
            if dbg is not None:
                T.dma("sp", dbg, xnT[:], reads=xnT_b)
                return
            src = hsrc
            for (fa, fb) in parts:
                with contextlib.ExitStack() as es2:
                    ps_g = P.ring([0, 1, 2, 3])
                    ps_u = P.ring([4, 5, 6, 7])
                    f = fa
                    while f < fb:
                        nf = min(2, fb - f)
                        sets = {}
                        for gu, pr in ((0, ps_g), (1, ps_u)):
                            cbase = gu * DFF + f * 128
                            banks = [[pr.next() for th in range(2)] for fc in range(nf)]
                            sets[gu] = banks
                            for kg in range(NCH // 8):
                                wt, wt_b = wr.next()
                                T.dma("pool", wt[:, :, 0:nf * 128],
                                      w_in[kg * 1024:(kg + 1) * 1024, cbase:cbase + nf * 128].rearrange("(k p) c -> p k c", p=128),
                                      writes=[wt_b])
                                for k8 in range(8):
                                    k = kg * 8 + k8
                                    for fc in range(nf):
                                        for th in range(2):
                                            ps, ps_b = banks[fc][th]
                                            T.op("pe", lambda e: e.matmul(out=ps[:], lhsT=wt[:, k8, fc * 128:(fc + 1) * 128],
                                                                          rhs=xnT[:, k, th * 512:(th + 1) * 512],
                                                                          start=(k == 0), stop=(k == NCH - 1)),
                                                 reads=[wt_b] + xnT_b[th * 4:(th + 1) * 4], writes=[ps_b])
                        for fc in range(nf):
                            for th in range(2):
                                sg, sg_b = sg_r.next()
                                pg, pg_b = sets[0][fc][th]
                                pu, pu_b = sets[1][fc][th]
                                T.op("act", lambda e: e.activation(out=sg[:], in_=pg[:], func=AF.Silu),
                                     reads=[pg_b], writes=[sg_b])
                                T.op("dve", lambda e: e.tensor_tensor(out=actT[:, f - fa + fc, th * 512:(th + 1) * 512],
                                                                      in0=sg[:], in1=pu[:], op=ALU.mult),
                                     reads=[sg_b, pu_b], writes=[actT_b[f - fa + fc]])
                        f += nf
                with contextlib.ExitStack() as es2:
                    ps_r = P.ring(range(8))
                    tm_proj_residual(
                        nc, T, actT, lambda k, tt: actT_b[k], fb - fa,
                        lambda k0, k1, c0, c1: w_out[(fa + k0) * 128:(fa + k1) * 128, c0:c1].rearrange("(k p) c -> p k c", p=128),
                        src, hdst, hb[blk], blk, 0.5, wr, ps_r, hp_r)
                src = hdst
            T.barrier()


def tm_proj(nc, T, lhsT, rb, nk, w_rows, ncols, wr, ps_r, epi):
    KC = 8
    for ct in range(ncols // 256):
        pss = [ps_r.next() for _ in range(TT // 2)]
        ngrp = (nk + KC - 1) // KC
        for kg in range(ngrp):
            k0 = kg * KC
            k1 = min(nk, k0 + KC)
            wt, wt_b = wr.next()
            T.dma("pool", wt[:, 0:k1 - k0, :], w_rows(k0, k1, ct * 256, ct * 256 + 256), writes=[wt_b])
            for k in range(k0, k1):
                for tt in range(TT):
                    ps, ps_b = pss[tt // 2]
                    T.op("pe", lambda e: e.matmul(out=ps[:, (tt % 2) * 256:(tt % 2) * 256 + 256],
                                                  lhsT=lhsT[:, k, tt * 128:(tt + 1) * 128], rhs=wt[:, k - k0, :],
                                                  start=(k == 0 and tt % 2 == 0), stop=(k == nk - 1), skip_group_check=True),
                         reads=[wt_b, rb(k, tt)], writes=[ps_b])

        def get(tt):
            ps, ps_b = pss[tt // 2]
            return ps[:, (tt % 2) * 256:(tt % 2) * 256 + 256], ps_b
        epi(ct, get)


def mixer_proj_phase(nc, T, P, C, l, h, hb, prm, scr):
    w_in = prm["mix_w_in"][l]
    with contextlib.ExitStack() as es:
        xnT = es.enter_context(nc.sbuf_tensor(U("m_xnT"), [128, NCH, TB], BF16))
        xnT_b = [Buf(f"mxnT{t}") for t in range(TT)]
        g_sb = es.enter_context(nc.sbuf_tensor(U("m_g"), [128, NCH], F32))
        g_b = Buf("mg")
        T.dma("sp", g_sb[:], prm["mix_norm"][l], writes=[g_b])
        cs_sb = es.enter_context(nc.sbuf_tensor(U("m_cs"), [128, S // 128, 2, 2, 16], F32))
        cs_b = Buf("cs")
        T.dma("sp", cs_sb[:], C["rope"], writes=[cs_b])
        wr = Ring(nc, es, "m_w", [128, 8, 256], BF16, 4)
        st32 = Ring(nc, es, "m_st32", [128, TB], F32, 3)
        stq = Ring(nc, es, "m_stq", [128, TT, 256], BF16, 2)
        tmp = Ring(nc, es, "m_tmp", [128, 4, 2, 16], F32, 2)
        xs_r = Ring(nc, es, "m_xs", [128, 256], F32, 3)
        for blk in range(NTB):
            t0 = blk * TB
            with contextlib.ExitStack() as es2:
                norm_transpose_block(nc, T, es2, P, h, hb[blk], blk, g_sb, xnT, xnT_b, C["ident_bf"], g_b, C["eps"], C["cb"])
                T.barrier()
            ps_sets = [P.ring([0, 1, 2, 3]), P.ring([4, 5, 6, 7])]
            units = [(f * 128, 2, False) for f in range(0, 16, 2)] + [(5120, 1, True)]
            if "nofm" in _DBG:
                units = []
            if "noff" in _DBG:
                units = units[:-1]
            for ui, (cbase, nf, is_ff) in enumerate(units):
                pr = ps_sets[ui % 2]
                ncol = nf * 128
                nfc = nf
                banks = [[pr.next() for th in range(2)] for fc in range(nfc)]
                for kg in range(NCH // 8):
                    wt, wt_b = wr.next()
                    T.dma("pool", wt[:, :, 0:ncol],
                          w_in[kg * 1024:(kg + 1) * 1024, cbase:cbase + ncol].rearrange("(k p) c -> p k c", p=128), writes=[wt_b])
                    for k8 in range(8):
                        k = kg * 8 + k8
                        for fc in range(nfc):
                            m = 128
                            for th in range(2):
                                ps, ps_b = banks[fc][th]
                                T.op("pe", lambda e: e.matmul(out=ps[0:m, :], lhsT=wt[:, k8, fc * 128:fc * 128 + m],
                                                              rhs=xnT[:, k, th * 512:(th + 1) * 512],
                                                              start=(k == 0), stop=(k == NCH - 1)),
                                     reads=[wt_b] + xnT_b[th * 4:(th + 1) * 4], writes=[ps_b])
                for fc in range(nfc):
                    m = 8 if is_ff else 128
                    sg, sg_b = st32.next()
                    for th in range(2):
                        ps, ps_b = banks[fc][th]
                        eng = "act" if th == 0 else "dve"
                        if eng == "act":
                            T.op("act", lambda e: e.activation(out=sg[0:m, th * 512:(th + 1) * 512], in_=ps[0:m, :], func=AF.Copy),
                                 reads=[ps_b], writes=[sg_b])
                        else:
                            T.op("dve", lambda e: e.tensor_copy(out=sg[0:m, th * 512:(th + 1) * 512], in_=ps[0:m, :]),
                                 reads=[ps_b], writes=[sg_b])
                    if not is_ff:
                        r0 = cbase + fc * 128
                        T.dma("sp", scr["xgT"][r0:r0 + 128, t0:t0 + TB], sg[:], reads=[sg_b], writes=[scr["xgT_b"][r0 // 128]])
                    else:
                        T.dma("sp", scr["ffT"][:, t0:t0 + TB], sg[0:8, :], reads=[sg_b], writes=[scr["ffT_b"]])
            ps_r = P.ring(range(8))
            for (wc0, ncols, oc0, rope_cols) in (((2048, 3072, 0, 0), (5128, 6144, 3072, 4096)) if "notm" not in _DBG else ()):
                def epi(ct, get, wc0=wc0, oc0=oc0, rope_cols=rope_cols):
                    sq, sq_b = stq.next()
                    do_rope = ct * 256 < rope_cols and "norope" not in _DBG
                    for tt in range(TT):
                        ps, ps_b = get(tt)
                        if not do_rope:
                            if tt % 2 == 0:
                                T.op("act", lambda e: e.activation(out=sq[:, tt, :], in_=ps, func=AF.Copy), reads=[ps_b], writes=[sq_b])
                            else:
                                T.op("dve", lambda e: e.tensor_copy(out=sq[:, tt, :], in_=ps), reads=[ps_b], writes=[sq_b])
                            continue
                        gt = blk * TT + tt
                        xs, xs_b = xs_r.next()
                        T.op("act", lambda e: e.activation(out=xs[:], in_=ps, func=AF.Copy), reads=[ps_b], writes=[xs_b])
                        xv = xs[:].rearrange("p (h e) -> p h e", h=2)
                        x1 = xv[:, :, 0:16]
                        x2 = xv[:, :, 16:32]
                        cc = cs_sb[:, gt, 0, :, :]
                        ss = cs_sb[:, gt, 1, :, :]
                        tm, tm_b = tmp.next()
                        T.op("dve", lambda e: e.tensor_tensor(out=tm[:, 0], in0=x1, in1=cc, op=ALU.mult), reads=[xs_b, cs_b], writes=[tm_b])
                        T.op("dve", lambda e: e.tensor_tensor(out=tm[:, 1], in0=x2, in1=ss, op=ALU.mult), reads=[xs_b, cs_b], writes=[tm_b])
                        T.op("dve", lambda e: e.tensor_tensor(out=tm[:, 2], in0=x2, in1=cc, op=ALU.mult), reads=[xs_b, cs_b], writes=[tm_b])
                        T.op("dve", lambda e: e.tensor_tensor(out=tm[:, 3], in0=x1, in1=ss, op=ALU.mult), reads=[xs_b, cs_b], writes=[tm_b])
                        T.op("dve", lambda e: e.tensor_tensor(out=x1, in0=tm[:, 0], in1=tm[:, 1], op=ALU.subtract),
                             reads=[tm_b], writes=[xs_b])
                        T.op("dve", lambda e: e.tensor_tensor(out=x2, in0=tm[:, 2], in1=tm[:, 3], op=ALU.add),
                             reads=[tm_b], writes=[xs_b])
                        T.op("act", lambda e: e.activation(out=sq[:, tt, :], in_=xs[:], func=AF.Copy), reads=[xs_b], writes=[sq_b])
                    c0 = oc0 + ct * 256
                    ov = scr["qkv"][t0:t0 + TB, c0:c0 + 256].rearrange("(t p) c -> p t c", p=128)
                    T.dma("sp", ov, sq[:], reads=[sq_b], writes=[scr["qkv_b"]])
                tm_proj(nc, T, xnT, lambda k, tt: xnT_b[tt], NCH,
                        lambda k0, k1, c0, c1, wc0=wc0: w_in[k0 * 128:k1 * 128, wc0 + c0:wc0 + c1].rearrange("(k p) c -> p k c", p=128),
                        ncols, wr, ps_r, epi)
        T.barrier()


def lru_phase(nc, T, P, C, l, prm, scr):
    with contextlib.ExitStack() as es:
        NG = 8
        wa = es.enter_context(nc.sbuf_tensor(U("l_wa"), [128, NG, 128], BF16))
        wx = es.enter_context(nc.sbuf_tensor(U("l_wx"), [128, NG, 128], BF16))
        pv = es.enter_context(nc.sbuf_tensor(U("l_pv"), [128, 9, NG], F32))
        dv = es.enter_context(nc.sbuf_tensor(U("l_dv"), [128, 4, NG], F32))
        cb = Buf("lru_const")
        T.dma("pool", wa[:], prm["lru_w_a"][l].rearrange("g i j -> i g j"), writes=[cb])
        T.dma("pool", wx[:], prm["lru_w_x"][l].rearrange("g i j -> i g j"), writes=[cb])
        T.dma("sp", pv[:], prm["lru_pv"][l], writes=[cb])
        T.op("act", lambda e: e.activation(out=dv[:, 0, :], in_=pv[:, 7, :], func=AF.Exp, scale=-1.0), reads=[cb], writes=[cb])
        T.op("act", lambda e: e.activation(out=dv[:, 1, :], in_=dv[:, 0, :], func=AF.Ln, bias=C["one"][:, 0:1]), reads=[cb, C["cb"]], writes=[cb])
        T.op("dve", lambda e: e.tensor_scalar(out=dv[:, 2, :], in0=dv[:, 1, :], scalar1=-8.0, scalar2=None, op0=ALU.mult), reads=[cb], writes=[cb])
        T.op("dve", lambda e: e.tensor_scalar(out=dv[:, 3, :], in0=dv[:, 1, :], scalar1=-16.0, scalar2=None, op0=ALU.mult), reads=[cb], writes=[cb])
        xa_r = Ring(nc, es, "l_xa", [128, S + 4], F32, 2)
        for t_ in xa_r.t:
            T.op("dve", lambda e: e.memset(t_[:, 0:4], 0.0), writes=[cb])
        ga_r = Ring(nc, es, "l_ga", [128, S], F32, 2)
        xc_r = Ring(nc, es, "l_xc", [128, S], F32, 1)
        xcb_r = Ring(nc, es, "l_xcb", [128, S], BF16, 1)
        r_r = Ring(nc, es, "l_r", [128, S], F32, 1)
        i_r = Ring(nc, es, "l_i", [128, S], F32, 1)
        a_r = Ring(nc, es, "l_a", [128, S], F32, 1)
        m_r = Ring(nc, es, "l_m", [128, S], F32, 1)
        t_r = Ring(nc, es, "l_t", [128, S], F32, 1)
        y_r = Ring(nc, es, "l_y", [128, S], F32, 2)
        ps_r = P.ring(range(8))
        gn = GroupNorm(nc, T, P, C, es, D_LRU)
        for g in range(NG):
            xa, xa_b = xa_r.next()
            ga, ga_b = ga_r.next()
            xc, xc_b = xc_r.next()
            xcb, xcb_b = xcb_r.next()
            r, r_b = r_r.next()
            ii, i_b = i_r.next()
            a, a_b = a_r.next()
            m, m_b = m_r.next()
            t, t_b = t_r.next()
            y, y_b = y_r.next()
            T.dma("sp", xa[:, 3:3 + S], scr["xgT"][g * 128:(g + 1) * 128, :], reads=[scr["xgT_b"][g], cb], writes=[xa_b])
            T.dma("sp", ga[:], scr["xgT"][(8 + g) * 128:(9 + g) * 128, :], reads=[scr["xgT_b"][8 + g]], writes=[ga_b])
            T.op("dve", lambda e: e.tensor_scalar(out=xc[:], in0=xa[:, 0:S], scalar1=pv[:, 0, g:g + 1], scalar2=pv[:, 4, g:g + 1],
                                                  op0=ALU.mult, op1=ALU.add), reads=[xa_b, cb], writes=[xc_b])
            for j in range(1, 4):
                T.op("dve", lambda e: e.scalar_tensor_tensor(out=xc[:], in0=xa[:, j:j + S], scalar=pv[:, j, g:g + 1], in1=xc[:],
                                                             op0=ALU.mult, op1=ALU.add), reads=[xa_b, cb, xc_b], writes=[xc_b])
            T.op("act", lambda e: e.activation(out=xcb[:], in_=xc[:], func=AF.Copy), reads=[xc_b], writes=[xcb_b])
            for (wt, bcol, dst, dst_b) in ((wa, 5, r, r_b), (wx, 6, ii, i_b)):
                for q in range(4):
                    ps, ps_b = ps_r.next()
                    T.op("pe", lambda e: e.matmul(out=ps[:], lhsT=wt[:, g, :], rhs=xcb[:, q * 512:(q + 1) * 512], start=True, stop=True),
                         reads=[cb, xcb_b], writes=[ps_b])
                    T.op("act", lambda e: e.activation(out=dst[:, q * 512:(q + 1) * 512], in_=ps[:], func=AF.Sigmoid,
                                                       bias=pv[:, bcol, g:g + 1]), reads=[ps_b, cb], writes=[dst_b])
            T.op("act", lambda e: e.activation(out=a[:], in_=r[:], func=AF.Exp, scale=dv[:, 2, g:g + 1]), reads=[r_b, cb], writes=[a_b])
            T.op("act", lambda e: e.activation(out=m[:], in_=r[:], func=AF.Exp, scale=dv[:, 3, g:g + 1]), reads=[r_b, cb], writes=[m_b])
            T.op("dve", lambda e: e.tensor_scalar(out=m[:], in0=m[:], scalar1=-1.0, scalar2=1.0, op0=ALU.mult, op1=ALU.add),
                 reads=[m_b], writes=[m_b])
            T.op("act", lambda e: e.activation(out=m[:], in_=m[:], func=AF.Sqrt), reads=[m_b], writes=[m_b])
            T.op("dve", lambda e: e.tensor_tensor(out=m[:], in0=m[:], in1=ii[:], op=ALU.mult), reads=[m_b, i_b], writes=[m_b])
            T.op("dve", lambda e: e.tensor_tensor(out=m[:], in0=m[:], in1=xc[:], op=ALU.mult), reads=[m_b, xc_b], writes=[m_b])
            T.op("dve", lambda e: e.tensor_tensor_scan(out=r[:], data0=a[:], data1=m[:], initial=0.0, op0=ALU.mult, op1=ALU.add),
                 reads=[a_b, m_b], writes=[r_b])
            T.op("dve", lambda e: e.tensor_tensor(out=t[:], in0=ga[:], in1=ga[:], op=ALU.mult), reads=[ga_b], writes=[t_b])
            T.op("dve", lambda e: e.tensor_scalar(out=t[:], in0=t[:], scalar1=0.044715, scalar2=1.0, op0=ALU.mult, op1=ALU.add),
                 reads=[t_b], writes=[t_b])
            T.op("dve", lambda e: e.tensor_tensor(out=t[:], in0=t[:], in1=ga[:], op=ALU.mult), reads=[t_b, ga_b], writes=[t_b])
            T.op("act", lambda e: e.activation(out=t[:], in_=t[:], func=AF.Sigmoid, scale=1.5957691216057308), reads=[t_b], writes=[t_b])
            T.op("dve", lambda e: e.tensor_tensor(out=t[:], in0=t[:], in1=ga[:], op=ALU.mult), reads=[t_b, ga_b], writes=[t_b])
            T.op("dve", lambda e: e.tensor_tensor(out=y[:], in0=t[:], in1=r[:], op=ALU.mult), reads=[t_b, r_b], writes=[y_b])
            gn.add(y, y_b, scr, g, first=(g == 0))
        gn.finish(scr, 0, NG, prm["gn_gain"][l])
        T.barrier()


class GroupNorm:
    def __init__(self, nc, T, P, C, es, width):
        self.nc, self.T, self.P, self.C = nc, T, P, C
        self.width = width
        self.acc = es.enter_context(nc.sbuf_tensor(U("gn_acc"), [128, S], F32))
        self.acc_b = Buf("gn_acc")
        self.sq_r = Ring(nc, es, "gn_sq", [128, S], F32, 1)
        self.ld_r = Ring(nc, es, "gn_ld", [128, S], F32, 2)
        self.o_r = Ring(nc, es, "gn_o", [128, S], BF16, 2)
        self.gain = es.enter_context(nc.sbuf_tensor(U("gn_gain"), [128, NCH], F32))
        self.gain_b = Buf("gn_gain")
        self.ps_r = P.ring([6, 7])

    def add(self, y, y_b, scr, chunk, first):
        T, C = self.T, self.C
        T.dma("sp", scr["yraw"][chunk * 128:(chunk + 1) * 128, :], y[:], reads=[y_b], writes=[scr["yraw_b"][chunk]])
        sq, sq_b = self.sq_r.next()
        T.op("act", lambda e: e.activation(out=sq[:], in_=y[:], func=AF.Square), reads=[y_b], writes=[sq_b])
        for q in range(4):
            ps, ps_b = self.ps_r.next()
            T.op("pe", lambda e: e.matmul(out=ps[:], lhsT=C["ones_f"][:], rhs=sq[:, q * 512:(q + 1) * 512], start=True, stop=True),
                 reads=[C["cb"], sq_b], writes=[ps_b])
            if first:
                T.op("dve", lambda e: e.tensor_copy(out=self.acc[:, q * 512:(q + 1) * 512], in_=ps[:]), reads=[ps_b], writes=[self.acc_b])
            else:
                T.op("dve", lambda e: e.tensor_tensor(out=self.acc[:, q * 512:(q + 1) * 512], in0=self.acc[:, q * 512:(q + 1) * 512],
                                                      in1=ps[:], op=ALU.add), reads=[ps_b, self.acc_b], writes=[self.acc_b])

    def finish(self, scr, chunk0, nchunks, gain_pc):
        T, C = self.T, self.C
        T.dma("sp", self.gain[:], gain_pc, writes=[self.gain_b])
        T.op("act", lambda e: e.activation(out=self.acc[:], in_=self.acc[:], func=AF.Sqrt, scale=1.0 / self.width, bias=C["eps"][:, 0:1]),
             reads=[self.acc_b, C["cb"]], writes=[self.acc_b])
        T.op("dve", lambda e: e.reciprocal(out=self.acc[:], in_=self.acc[:]), reads=[self.acc_b], writes=[self.acc_b])
        for c in range(chunk0, chunk0 + nchunks):
            ld, ld_b = self.ld_r.next()
            o, o_b = self.o_r.next()
            T.dma("sp", ld[:], scr["yraw"][c * 128:(c + 1) * 128, :], reads=[scr["yraw_b"][c]], writes=[ld_b])
            T.op("dve", lambda e: e.scalar_tensor_tensor(out=o[:], in0=ld[:], scalar=self.gain[:, c:c + 1], in1=self.acc[:],
                                                         op0=ALU.mult, op1=ALU.mult), reads=[ld_b, self.gain_b, self.acc_b], writes=[o_b])
            T.dma("sp", scr["yT"][c * 128:(c + 1) * 128, :], o[:], reads=[o_b], writes=[scr["yT_b"]])


def attn_phase(nc, T, P, C, l, prm, scr, fox):
    scale = HD ** -0.5
    nh = N_FOX if fox else N_DIL
    qc0, kc0, vc0 = (0, 1024, 2048) if fox else (3072, 5120, 7168)
    chunk0 = 8 if fox else 16
    with contextlib.ExitStack() as es:
        cb = Buf("attn_const")
        mask = es.enter_context(nc.sbuf_tensor(U("a_mask"), [128, S if not fox else 128], F32))
        T.dma("sp", mask[:], C["mask_causal"] if fox else C["mask_dil"], writes=[cb])
        if fox:
            ff = es.enter_context(nc.sbuf_tensor(U("a_ff"), [128, S], F32))
            cum = es.enter_context(nc.sbuf_tensor(U("a_cum"), [128, S], F32))
            onesr = es.enter_context(nc.sbuf_tensor(U("a_onesr"), [128, S], F32))
            fb = es.enter_context(nc.sbuf_tensor(U("a_fb"), [8, 2], F32))
            sel = es.enter_context(nc.sbuf_tensor(U("a_sel"), [128, 8, 128], F32))
            T.op("dve", lambda e: e.memset(cum[:], 0.0), writes=[cb])
            T.dma("sp", ff[0:8, :], scr["ffT"], reads=[scr["ffT_b"]], writes=[cb])
            T.dma("sp", fb[:, 0:1], prm["fox_b_f"][l], writes=[cb])
            T.dma("sp", sel[:], C["sel"], writes=[cb])
            T.op("dve", lambda e: e.memset(onesr[:], 1.0), writes=[cb])
            T.op("dve", lambda e: e.tensor_scalar(out=fb[:, 1:2], in0=fb[:, 0:1], scalar1=-1.0, scalar2=None, op0=ALU.mult), reads=[cb], writes=[cb])
            T.op("act", lambda e: e.activation(out=ff[0:8, :], in_=ff[0:8, :], func=AF.Exp, scale=-1.0, bias=fb[:, 1:2]), reads=[cb], writes=[cb])
            T.op("act", lambda e: e.activation(out=ff[0:8, :], in_=ff[0:8, :], func=AF.Ln, bias=C["one"][0:8, 0:1]), reads=[cb, C["cb"]], writes=[cb])
            T.op("dve", lambda e: e.tensor_tensor_scan(out=cum[0:8, :], data0=onesr[0:8, :], data1=ff[0:8, :], initial=0.0, op0=ALU.mult, op1=ALU.add),
                 reads=[cb], writes=[cb])
            nd_r = Ring(nc, es, "a_nd", [128, S], F32, 2)
        qk_r = Ring(nc, es, "a_qk", [128, 2, S // 128, 128], BF16, 2)
        v_r = Ring(nc, es, "a_v", [128, S // 128, 128], BF16, 2)
        qkT_r = Ring(nc, es, "a_qkT", [128, 2, S], BF16, 2)
        z_r = Ring(nc, es, "a_z", [128, S], F32, 2)
        p_r = Ring(nc, es, "a_p", [128, S], BF16, 3)
        pT_r = Ring(nc, es, "a_pT", [128, S // 128, 128], BF16, 2)
        st_r = Ring(nc, es, "a_st", [128, 4], F32, 5)
        dg_r = Ring(nc, es, "a_dg", [128, 128], BF16, 2)
        y_r = Ring(nc, es, "a_y", [128, S], F32, 2)
        s_ps = P.ring([0, 1, 2, 3])
        t_ps = P.ring([4, 5])
        o_ps = P.ring([6, 7])
        gn = GroupNorm(nc, T, P, C, es, D_FOX if fox else D_DIL)

        def head_setup(hh):
            qk, qk_b = qk_r.next()
            v, v_b = v_r.next()
            qkT, qkT_b = qkT_r.next()
            y, y_b = y_r.next()
            for j, c0 in enumerate((qc0, kc0)):
                T.dma("sp", qk[:, j], scr["qkv"][:, c0 + hh * 128:c0 + (hh + 1) * 128].rearrange("(t p) e -> p t e", p=128),
                      reads=[scr["qkv_b"]], writes=[qk_b])
            T.dma("sp", v[:], scr["qkv"][:, vc0 + hh * 128:vc0 + (hh + 1) * 128].rearrange("(t p) e -> p t e", p=128),
                  reads=[scr["qkv_b"]], writes=[v_b])
            for j in range(2):
                for t8 in range(2):
                    psf, ps_b = s_ps.next()
                    tp = psf[:].bitcast(BF16).rearrange("p (j c) -> p j c", c=128)
                    for i in range(8):
                        T.op("pe", lambda e: e.transpose(out=tp[:, i, :], in_=qk[:, j, t8 * 8 + i, :], identity=C["ident_bf"][:]),
                             reads=[qk_b, C["cb"]], writes=[ps_b])
                    if t8 == 0:
                        T.op("act", lambda e: e.activation(out=qkT[:, j, t8 * 1024:(t8 + 1) * 1024], in_=psf[:].bitcast(BF16), func=AF.Copy),
                             reads=[ps_b], writes=[qkT_b])
                    else:
                        T.op("dve", lambda e: e.tensor_copy(out=qkT[:, j, t8 * 1024:(t8 + 1) * 1024], in_=psf[:].bitcast(BF16)),
                             reads=[ps_b], writes=[qkT_b])
            nd = nd_b = None
            if fox:
                nd, nd_b = nd_r.next()
                for q in range(4):
                    ps, ps_b = s_ps.next()
                    T.op("pe", lambda e: e.matmul(out=ps[:], lhsT=sel[:, hh, :], rhs=cum[:, q * 512:(q + 1) * 512], start=True, stop=True),
                         reads=[cb], writes=[ps_b])
                    T.op("act", lambda e: e.activation(out=nd[:, q * 512:(q + 1) * 512], in_=ps[:], func=AF.Copy), reads=[ps_b], writes=[nd_b])
            return dict(hh=hh, v=v, v_b=v_b, qkT=qkT, qkT_b=qkT_b, y=y, y_b=y_b, nd=nd, nd_b=nd_b)

        def stage_1a(H, qb):
            nk = (qb + 1) * 128
            z, z_b = z_r.next()
            p, p_b = p_r.next()
            st, st_b = st_r.next()
            qkT, qkT_b = H["qkT"], H["qkT_b"]
            banks = []
            for c in range((nk + 511) // 512):
                w = min(512, nk - c * 512)
                ps, ps_b = s_ps.next()
                banks.append((c, w, ps, ps_b))
                T.op("pe", lambda e: e.matmul(out=ps[:, 0:w], lhsT=qkT[:, 0, qb * 128:(qb + 1) * 128], rhs=qkT[:, 1, c * 512:c * 512 + w],
                                              start=True, stop=True), reads=[qkT_b], writes=[ps_b])
            return dict(H=H, qb=qb, nk=nk, z=z, z_b=z_b, p=p, p_b=p_b, st=st, st_b=st_b, banks=banks)

        def stage_1b(I):
            H, qb, nk, z, z_b, st, st_b = I["H"], I["qb"], I["nk"], I["z"], I["z_b"], I["st"], I["st_b"]
            for (c, w, ps, ps_b) in I["banks"]:
                if fox:
                    T.op("dve", lambda e: e.scalar_tensor_tensor(out=z[:, c * 512:c * 512 + w], in0=ps[:, 0:w], scalar=scale,
                                                                 in1=H["nd"][:, c * 512:c * 512 + w], op0=ALU.mult, op1=ALU.add),
                         reads=[ps_b, H["nd_b"]], writes=[z_b])
                else:
                    m0 = (15 - qb) * 128 + c * 512
                    T.op("dve", lambda e: e.scalar_tensor_tensor(out=z[:, c * 512:c * 512 + w], in0=ps[:, 0:w], scalar=scale,
                                                                 in1=mask[:, m0:m0 + w], op0=ALU.mult, op1=ALU.add),
                         reads=[ps_b, cb], writes=[z_b])
            if fox:
                T.op("dve", lambda e: e.tensor_tensor(out=z[:, nk - 128:nk], in0=z[:, nk - 128:nk], in1=mask[:], op=ALU.add),
                     reads=[z_b, cb], writes=[z_b])
            T.op("dve", lambda e: e.reduce_max(out=st[:, 0:1], in_=z[:, 0:nk], axis=AX.X), reads=[z_b], writes=[st_b])
            T.op("dve", lambda e: e.tensor_scalar(out=st[:, 1:2], in0=st[:, 0:1], scalar1=-1.0, scalar2=None, op0=ALU.mult),
                 reads=[st_b], writes=[st_b])

        def stage_2(I):
            nk, z, z_b, p, p_b, st, st_b = I["nk"], I["z"], I["z_b"], I["p"], I["p_b"], I["st"], I["st_b"]
            T.op("act", lambda e: e.activation(out=p[:, 0:nk], in_=z[:, 0:nk], func=AF.Exp, bias=st[:, 1:2], accum_out=st[:, 2:3]),
                 reads=[z_b, st_b], writes=[p_b, st_b])

        def stage_3a(I):
            qb, p, p_b, st, st_b = I["qb"], I["p"], I["p_b"], I["st"], I["st_b"]
            pT, pT_b = pT_r.next()
            dg, dg_b = dg_r.next()
            I["pT"], I["pT_b"] = pT, pT_b
            T.op("dve", lambda e: e.reciprocal(out=st[:, 3:4], in_=st[:, 2:3]), reads=[st_b], writes=[st_b])
            T.op("dve", lambda e: e.tensor_scalar(out=dg[:], in0=C["ident_f"][:], scalar1=st[:, 3:4], scalar2=None, op0=ALU.mult),
                 reads=[st_b, C["cb"]], writes=[dg_b])
            for k4 in range((qb + 4) // 4):
                nb = min(4, qb + 1 - k4 * 4)
                ps, ps_b = t_ps.next()
                for i in range(nb):
                    kb = k4 * 4 + i
                    T.op("pe", lambda e: e.matmul(out=ps[:, i * 128:(i + 1) * 128], lhsT=p[:, kb * 128:(kb + 1) * 128], rhs=dg[:],
                                                  start=True, stop=True), reads=[p_b, dg_b], writes=[ps_b])
                T.op("act", lambda e: e.activation(out=pT[:, k4 * 4:k4 * 4 + nb, :], in_=ps[:, 0:nb * 128].rearrange("p (i c) -> p i c", c=128),
                                                   func=AF.Copy), reads=[ps_b], writes=[pT_b])

        def stage_3b(I):
            H, qb, pT, pT_b = I["H"], I["qb"], I["pT"], I["pT_b"]
            po, po_b = o_ps.next()
            v, v_b = H["v"], H["v_b"]
            for kb in range(qb + 1):
                T.op("pe", lambda e: e.matmul(out=po[:, 0:128], lhsT=v[:, kb, :], rhs=pT[:, kb, :], start=(kb == 0), stop=(kb == qb)),
                     reads=[v_b, pT_b], writes=[po_b])
            T.op("act", lambda e: e.activation(out=H["y"][:, qb * 128:(qb + 1) * 128], in_=po[:, 0:128], func=AF.Copy),
                 reads=[po_b], writes=[H["y_b"]])
            if qb == S // 128 - 1:
                gn.add(H["y"], H["y_b"], scr, chunk0 + H["hh"], first=(H["hh"] == 0))

        items = [(hh, qb) for hh in range(nh) for qb in range(S // 128)]
        n = len(items)
        live = {}
        Hcur = None
        for step in range(n + 2):
            k3, k1, k2 = step - 2, step, step - 1
            if k1 < n:
                hh, qb = items[k1]
                if qb == 0:
                    Hcur = head_setup(hh)
                live[k1] = stage_1a(Hcur, qb)
            if k3 >= 0:
                stage_3a(live[k3])
            if k1 < n:
                stage_1b(live[k1])
            if 0 <= k2 < n:
                stage_2(live[k2])
            if k3 >= 0:
                stage_3b(live[k3])
                del live[k3]
        gn.finish(scr, chunk0, nh, prm["gn_gain"][l])
        T.barrier()


def outproj_phase(nc, T, P, C, l, hsrc, hdst, hb, prm, scr):
    w_out = prm["mix_w_out"][l]
    with contextlib.ExitStack() as es:
        yT = es.enter_context(nc.sbuf_tensor(U("o_yT"), [128, NCH, TB], BF16))
        yT_b = Buf("o_yT")
        wr = Ring(nc, es, "o_w", [128, 8, 256], BF16, 4)
        hp_r = Ring(nc, es, "o_hp", [128, TT, 256], F32, 2)
        ps_r = P.ring(range(8))
        for blk in range(NTB):
            t0 = blk * TB
            T.dma("sp", yT[:], scr["yT"][:, t0:t0 + TB].rearrange("(c p) t -> p c t", p=128), reads=[scr["yT_b"]], writes=[yT_b])
            tm_proj_residual(nc, T, yT, lambda k, tt: yT_b, NCH,
                             lambda k0, k1, c0, c1: w_out[k0 * 128:k1 * 128, c0:c1].rearrange("(k p) c -> p k c", p=128),
                             hsrc, hdst, hb[blk], blk, 1.0, wr, ps_r, hp_r)
        T.barrier()


def final_norm_phase(nc, T, P, C, h, hb, out, g_row_bc):
    with contextlib.ExitStack() as es:
        g = es.enter_context(nc.sbuf_tensor(U("fn_g"), [128, D], F32))
        g_b = Buf("fn_g")
        T.dma("sp", g[:], g_row_bc, writes=[g_b])
        hs_r = Ring(nc, es, "fn_hs", [128, D], F32, 2)
        o_r = Ring(nc, es, "fn_o", [128, D], F32, 2)
        st_r = Ring(nc, es, "fn_st", [128, 4], F32, 2)
        ob = Buf("out")
        for tt in range(S // 128):
            hs, hs_b = hs_r.next()
            o, o_b = o_r.next()
            st, st_b = st_r.next()
            T.dma("sp", hs[:], h[tt * 128:(tt + 1) * 128, :], reads=hb[tt // TT], writes=[hs_b])
            T.op("act", lambda e: e.activation(out=o[:], in_=hs[:], func=AF.Square, accum_out=st[:, 0:1]), reads=[hs_b], writes=[o_b, st_b])
            T.op("act", lambda e: e.activation(out=st[:, 1:2], in_=st[:, 0:1], func=AF.Sqrt, scale=1.0 / D, bias=C["eps"][:, 0:1]),
                 reads=[st_b, C["cb"]], writes=[st_b])
            T.op("dve", lambda e: e.reciprocal(out=st[:, 2:3], in_=st[:, 1:2]), reads=[st_b], writes=[st_b])
            T.op("dve", lambda e: e.scalar_tensor_tensor(out=o[:], in0=hs[:], scalar=st[:, 2:3], in1=g[:], op0=ALU.mult, op1=ALU.mult),
                 reads=[hs_b, st_b, g_b], writes=[o_b])
            T.dma("sp", out[tt * 128:(tt + 1) * 128, :], o[:], reads=[o_b], writes=[ob])
        T.barrier()


PARAMS = (("ffn1_norm", [DEPTH, 128, NCH]), ("ffn1_w_in", [DEPTH, D, 2 * DFF]), ("ffn1_w_out", [DEPTH, DFF, D]),
          ("mix_norm", [DEPTH, 128, NCH]), ("mix_w_in", [DEPTH, D, D_IN]), ("lru_w_a", [DEPTH, 8, 128, 128]),
          ("lru_w_x", [DEPTH, 8, 128, 128]), ("lru_pv", [DEPTH, 128, 9, 8]), ("fox_b_f", [DEPTH, 8, 1]),
          ("gn_gain", [DEPTH, 128, NCH]), ("mix_w_out", [DEPTH, D, D]),
          ("ffn2_norm", [DEPTH, 128, NCH]), ("ffn2_w_in", [DEPTH, D, 2 * DFF]), ("ffn2_w_out", [DEPTH, DFF, D]),
          ("final_norm", [128, D]))
CONSTS = (("ident", [128, 128]), ("rope", [128, S // 128, 2, 2, 16]), ("mask_dil", [128, S]), ("mask_causal", [128, 128]),
          ("sel", [128, 8, 128]))


def build_program(phases="all", debug=False):
    nc = bass.Bass("TRN2", target_bir_lowering=False)
    x = nc.dram_tensor("x", [S, D], F32, kind="ExternalInput").ap()
    prm = {}
    for name, shape in PARAMS:
        prm[name] = nc.dram_tensor(name, shape, F32, kind="ExternalInput").ap()
    cin = {}
    for name, shape in CONSTS:
        cin[name] = nc.dram_tensor(name, shape, F32, kind="ExternalInput").ap()
    out = nc.dram_tensor("out", [S, D], F32, kind="ExternalOutput").ap()
    h = nc.dram_tensor("h_scr", [S, D], F32, kind="Internal").ap()
    hb = [[Buf(f"h{b}_{c}") for c in range(D // 256)] for b in range(NTB)]
    sk = "ExternalOutput" if debug else "Internal"
    sk2 = "ExternalOutput" if debug == 2 else "Internal"
    scr = {
        "xgT": nc.dram_tensor("s_xgT", [2048, S], F32, kind=sk).ap(), "xgT_b": [Buf() for _ in range(16)],
        "ffT": nc.dram_tensor("s_ffT", [8, S], F32, kind=sk).ap(), "ffT_b": Buf(),
        "qkv": nc.dram_tensor("s_qkv", [S, 9216], BF16, kind=sk2).ap(), "qkv_b": Buf(),
        "yraw": nc.dram_tensor("s_yraw", [D, S], F32, kind=sk).ap(), "yraw_b": [Buf() for _ in range(NCH)],
        "yT": nc.dram_tensor("s_yT", [D, S], BF16, kind=sk2).ap(), "yT_b": Buf(),
    }

    with contextlib.ExitStack() as es:
        T = TK(nc, es)
        P = PS(nc, es)
        C = {"cb": Buf("const")}
        cst = C["cb"]
        C["ident_f"] = es.enter_context(nc.sbuf_tensor(U("ident_f"), [128, 128], F32))
        C["ident_bf"] = es.enter_context(nc.sbuf_tensor(U("ident_bf"), [128, 128], BF16))
        C["ones_f"] = es.enter_context(nc.sbuf_tensor(U("ones_f"), [128, 128], F32))
        C["eps"] = es.enter_context(nc.sbuf_tensor(U("eps_sb"), [128, 1], F32))
        C["one"] = es.enter_context(nc.sbuf_tensor(U("one_sb"), [128, 1], F32))
        for k in ("rope", "mask_dil", "mask_causal", "sel"):
            C[k] = cin[k]
        T.dma("sp", C["ident_f"][:], cin["ident"], writes=[cst])
        T.op("dve", lambda e: e.tensor_copy(out=C["ident_bf"][:], in_=C["ident_f"][:]), reads=[cst], writes=[cst])
        T.op("dve", lambda e: e.memset(C["ones_f"][:], 1.0), writes=[cst])
        T.op("dve", lambda e: e.memset(C["eps"][:], EPS), writes=[cst])
        T.op("dve", lambda e: e.memset(C["one"][:], 1.0), writes=[cst])

        def mixer(l, hsrc, hdst, sub=(1, 2, 3, 4, 5)):
            if 1 in sub:
                mixer_proj_phase(nc, T, P, C, l, hsrc, hb, prm, scr)
            if 2 in sub:
                lru_phase(nc, T, P, C, l, prm, scr)
            if 3 in sub:
                attn_phase(nc, T, P, C, l, prm, scr, fox=True)
            if 4 in sub:
                attn_phase(nc, T, P, C, l, prm, scr, fox=False)
            if 5 in sub:
                outproj_phase(nc, T, P, C, l, hsrc, hdst, hb, prm, scr)

        if phases == "ffn1":
            ffn_phase(nc, T, P, C, x, out, hb, prm["ffn1_norm"][0], prm["ffn1_w_in"][0], prm["ffn1_w_out"][0])
        elif phases.startswith("mix"):
            mixer(0, x, out, sub=tuple(int(c) for c in phases[3:]) if len(phases) > 3 else (1, 2, 3, 4, 5))
        elif phases == "all":
            src = x
            for l in range(DEPTH):
                ffn_phase(nc, T, P, C, src, h, hb, prm["ffn1_norm"][l], prm["ffn1_w_in"][l], prm["ffn1_w_out"][l])
                src = h
                mixer(l, h, h)
                ffn_phase(nc, T, P, C, h, h, hb, prm["ffn2_norm"][l], prm["ffn2_w_in"][l], prm["ffn2_w_out"][l])
            final_norm_phase(nc, T, P, C, h, hb, out, prm["final_norm"])
        T.barrier()
        print("instructions", T.ninst, "waits", T.nwaits)
    return nc


def pc_layout(v):
    sh = v.shape
    return np.ascontiguousarray(v.reshape(sh[:-1] + (sh[-1] // 128, 128)).swapaxes(-1, -2))


def make_consts():
    c = {"ident": np.eye(128, dtype=np.float32)}
    inv = (1.0 / (np.float32(500000.0) ** (np.arange(0, 32, 2, dtype=np.float32) / np.float32(32)))).astype(np.float32)
    pos = np.arange(S, dtype=np.float32)
    ang = (pos[:, None] * inv[None, :]).astype(np.float32)
    cs = np.stack([np.cos(ang), np.sin(ang)], axis=1).astype(np.float32)
    cs = cs.reshape(S // 128, 128, 2, 1, 16).transpose(1, 0, 2, 3, 4)
    c["rope"] = np.ascontiguousarray(np.broadcast_to(cs, (128, S // 128, 2, 2, 16))).astype(np.float32)
    p = np.arange(128)[:, None]
    u = np.arange(S)[None, :]
    rel = p + (S - 128) - u
    cnt = ((rel >= 0) & (rel <= 128)).astype(np.float32) + ((rel >= 0) & (rel % 4 == 0) & (rel <= 512)) \
        + ((rel >= 0) & (rel % 16 == 0) & (rel <= 2048))
    c["mask_dil"] = np.where(cnt > 0, np.log(np.maximum(cnt, 1.0)), NEG).astype(np.float32)
    j = np.arange(128)[None, :]
    c["mask_causal"] = np.where(j <= p, 0.0, NEG).astype(np.float32)
    sel = np.zeros((128, 8, 128), np.float32)
    for i in range(8):
        sel[i, i, :] = 1.0
    c["sel"] = sel
    return c


def layout_params(inp):
    f = lambda k: np.ascontiguousarray(np.asarray(inp[k], dtype=np.float32))
    o = {}
    for k in ("ffn1_w_in", "ffn1_w_out", "mix_w_in", "lru_w_a", "lru_w_x", "mix_w_out", "ffn2_w_in", "ffn2_w_out"):
        o[k] = f(k)
    for k in ("ffn1_norm", "mix_norm", "ffn2_norm"):
        o[k] = pc_layout(f(k))
    cw = f("conv_w")
    rows = [cw[:, j] for j in range(4)] + [f("conv_b"), f("lru_b_a"), f("lru_b_x"), f("lru_lam"), f("out_norm_lru")]
    pv = np.stack(rows, axis=1)
    o["lru_pv"] = np.ascontiguousarray(pv.reshape(DEPTH, 9, 8, 128).transpose(0, 3, 1, 2))
    o["fox_b_f"] = f("fox_b_f").reshape(DEPTH, 8, 1)
    o["gn_gain"] = pc_layout(np.concatenate([f("out_norm_lru"), f("out_norm_fox"), f("out_norm_dil")], axis=-1))
    o["final_norm"] = np.ascontiguousarray(np.broadcast_to(f("final_norm")[None, :], (128, D)))
    return o


_NC_CACHE = {}


def kernel(**inputs):
    n = 8
    if "all" not in _NC_CACHE:
        _NC_CACHE["all"] = build_program("all")
    nc = _NC_CACHE["all"]
    shared = layout_params(inputs)
    shared.update(make_consts())
    x = np.asarray(inputs["x"], dtype=np.float32)
    in_maps = []
    for b in range(n):
        m = dict(shared)
        m["x"] = np.ascontiguousarray(x[b])
        in_maps.append(m)
    res = run_bass_kernel_spmd(nc, in_maps, core_ids=list(range(n)))
    return np.stack([np.asarray(r["out"]) for r in res.results], axis=0).astype(np.float32)
```
